# Optimizing a Trainium2 kernel written in Bass

```python
import jax, jax.numpy as jnp
from jax import lax
import numpy as np

D_MODEL = 1024
BATCH = 4
SEQ = 4096
DEPTH = 2

MIX_WIDTH = D_MODEL
GROUP_WIDTH = MIX_WIDTH // 4
ROPE_THETA = 500000.0
NORM_EPS = 1e-6

MLA_HEADS = 4
MLA_V = GROUP_WIDTH // MLA_HEADS
MLA_NOPE = 64
MLA_ROPE = 32
MLA_Q_RANK = D_MODEL // 4
MLA_KV_RANK = D_MODEL // 8
MLA_Q_BLOCK = 128

GLA_HEADS = 4
GLA_DV = GROUP_WIDTH // GLA_HEADS
GLA_DK = GLA_DV // 2
GLA_GATE_RANK = 16
GLA_TAU = 16.0
GLA_CHUNK = 64

MOBA_HEADS = 4
MOBA_HD = GROUP_WIDTH // MOBA_HEADS
MOBA_ROT = MOBA_HD // 4
MOBA_BLOCK = 256
MOBA_TOPK = 3
MOBA_Q_CHUNK = 64

LRU_WIDTH = GROUP_WIDTH
LRU_BLOCKS = 4
LRU_BW = LRU_WIDTH // LRU_BLOCKS
LRU_CONV = 4
LRU_C = 8.0

FFN_HIDDEN = -(-8 * D_MODEL // (3 * 256)) * 256

IN_SPLITS = (
    MLA_Q_RANK, MLA_KV_RANK, MLA_ROPE,
    GLA_HEADS * GLA_DK, GLA_HEADS * GLA_DK, GROUP_WIDTH, GLA_GATE_RANK, GROUP_WIDTH,
    GROUP_WIDTH, GROUP_WIDTH, GROUP_WIDTH,
    LRU_WIDTH, LRU_WIDTH,
)
IN_WIDTH = int(sum(IN_SPLITS))
SPLIT_POINTS = [int(s) for s in np.cumsum(IN_SPLITS)[:-1]]

kernel_name = "hybrid_parallel_mla_gla_moba_rglru"


def rms_norm(x, g):
    xf = x.astype(jnp.float32)
    y = xf * lax.rsqrt(jnp.mean(xf * xf, axis=-1, keepdims=True) + NORM_EPS)
    return (y * g.astype(jnp.float32)).astype(x.dtype)


def rope_tables(seq_len, dim):
    inv_freq = ROPE_THETA ** (-jnp.arange(0, dim, 2, dtype=jnp.float32) / dim)
    ang = jnp.arange(seq_len, dtype=jnp.float32)[:, None] * inv_freq[None, :]
    return jnp.cos(ang), jnp.sin(ang)


def apply_rope(x, cos, sin):
    half = x.shape[-1] // 2
    x1 = x[..., :half].astype(jnp.float32)
    x2 = x[..., half:].astype(jnp.float32)
    return jnp.concatenate([x1 * cos - x2 * sin, x2 * cos + x1 * sin], axis=-1).astype(x.dtype)


def causal_attention(q, k, v, scale):
    B, H, S, _ = q.shape
    dv = v.shape[-1]
    kpos = jnp.arange(S)

    def one(i):
        start = i * MLA_Q_BLOCK
        qb = lax.dynamic_slice_in_dim(q, start, MLA_Q_BLOCK, axis=2)
        s = jnp.einsum('bhqd,bhkd->bhqk', qb, k).astype(jnp.float32) * scale
        qpos = start + jnp.arange(MLA_Q_BLOCK)
        s = jnp.where(kpos[None, :] <= qpos[:, None], s, -jnp.inf)
        p = jax.nn.softmax(s, axis=-1).astype(v.dtype)
        return jnp.einsum('bhqk,bhkd->bhqd', p, v)

    o = lax.map(one, jnp.arange(S // MLA_Q_BLOCK))
    return jnp.moveaxis(o, 0, 2).reshape(B, H, S, dv)


def mla_mixer(c_q, c_kv, k_pe, q_norm_g, w_uq, kv_norm_g, w_ukv, cos, sin):
    B, S, _ = c_q.shape
    q = (rms_norm(c_q, q_norm_g) @ w_uq).reshape(B, S, MLA_HEADS, MLA_NOPE + MLA_ROPE).transpose(0, 2, 1, 3)
    q = jnp.concatenate([q[..., :MLA_NOPE], apply_rope(q[..., MLA_NOPE:], cos, sin)], axis=-1)
    kv = (rms_norm(c_kv, kv_norm_g) @ w_ukv).reshape(B, S, MLA_HEADS, MLA_NOPE + MLA_V).transpose(0, 2, 1, 3)
    k_pe = apply_rope(k_pe[:, None], cos, sin)
    k = jnp.concatenate([kv[..., :MLA_NOPE], jnp.broadcast_to(k_pe, (B, MLA_HEADS, S, MLA_ROPE))], axis=-1)
    v = kv[..., MLA_NOPE:]
    o = causal_attention(q, k, v, (MLA_NOPE + MLA_ROPE) ** -0.5)
    return o.transpose(0, 2, 1, 3).reshape(B, S, MLA_HEADS * MLA_V)


def gla_mixer(q, k, v, a_low, g, w_a2, b_a2, head_norm_g):
    B, S, _ = q.shape
    H, C = GLA_HEADS, GLA_CHUNK
    N = S // C

    def heads(t, d):
        return t.reshape(B, S, H, d).transpose(0, 2, 1, 3).reshape(B, H, N, C, d).astype(jnp.float32)

    log_a = jax.nn.log_sigmoid((a_low @ w_a2 + b_a2).astype(jnp.float32)) / GLA_TAU
    qf = heads(q, GLA_DK) * GLA_DK ** -0.5
    kf = heads(k, GLA_DK)
    vf = heads(v, GLA_DV)
    b = jnp.cumsum(heads(log_a, GLA_DK), axis=3)
    b_last = b[:, :, :, -1:, :]
    b_ref = 0.5 * b_last
    att = jnp.einsum('bhncd,bhnsd->bhncs', qf * jnp.exp(b - b_ref), kf * jnp.exp(b_ref - b))
    causal = jnp.tril(jnp.ones((C, C), dtype=bool))
    o_intra = jnp.einsum('bhncs,bhnsv->bhncv', jnp.where(causal, att, 0.0), vf)
    d_state = jnp.einsum('bhncd,bhncv->bhndv', kf * jnp.exp(b_last - b), vf)
    decay = jnp.exp(b_last[:, :, :, 0, :])

    def step(state, inp):
        dcy, ds = inp
        return dcy[..., None] * state + ds, state

    _, s_before = lax.scan(step, jnp.zeros((B, H, GLA_DK, GLA_DV), jnp.float32),
                           (jnp.moveaxis(decay, 2, 0), jnp.moveaxis(d_state, 2, 0)))
    s_before = jnp.moveaxis(s_before, 0, 2)
    o_inter = jnp.einsum('bhncd,bhndv->bhncv', qf * jnp.exp(b), s_before)
    o = (o_intra + o_inter).reshape(B, H, S, GLA_DV)
    o = rms_norm(o, head_norm_g).transpose(0, 2, 1, 3).reshape(B, S, H * GLA_DV)
    return (o * jax.nn.silu(g.astype(jnp.float32))).astype(g.dtype)


def moba_mixer(q, k, v, cos, sin):
    B, S, _ = q.shape
    H, D, BS, QC = MOBA_HEADS, MOBA_HD, MOBA_BLOCK, MOBA_Q_CHUNK

    def heads(t):
        return t.reshape(B, S, H, D).transpose(0, 2, 1, 3)

    def partial_rope(t):
        return jnp.concatenate([apply_rope(t[..., :MOBA_ROT], cos, sin), t[..., MOBA_ROT:]], axis=-1)

    q = partial_rope(heads(q))
    k = partial_rope(heads(k))
    v = heads(v)
    pad = (-S) % BS
    Sp = S + pad
    padw = ((0, 0), (0, 0), (0, pad), (0, 0))
    q, k, v = jnp.pad(q, padw), jnp.pad(k, padw), jnp.pad(v, padw)
    NB = Sp // BS
    topk = min(MOBA_TOPK, NB)
    k_blocks = k.reshape(B, H, NB, BS, D)
    v_blocks = v.reshape(B, H, NB, BS, D)
    k_mean = jnp.mean(k_blocks.astype(jnp.float32), axis=3)
    scale = D ** -0.5
    bi = jnp.arange(B)[:, None, None, None]
    hi = jnp.arange(H)[None, :, None, None]

    def one(i):
        start = i * QC
        blk = start // BS
        qb = lax.dynamic_slice_in_dim(q, start, QC, axis=2)
        gate = jnp.einsum('bhqd,bhnd->bhqn', qb.astype(jnp.float32), k_mean)
        gate = jnp.where(jnp.arange(NB) < blk, gate, -jnp.inf)
        _, idx = lax.top_k(gate, topk)
        valid = idx < blk
        k_sel = k_blocks[bi, hi, idx]
        v_sel = v_blocks[bi, hi, idx]
        s_sel = jnp.einsum('bhqd,bhqjkd->bhqjk', qb, k_sel).astype(jnp.float32) * scale
        s_sel = jnp.where(valid[..., None], s_sel, -jnp.inf).reshape(B, H, QC, topk * BS)
        k_own = lax.dynamic_slice_in_dim(k, blk * BS, BS, axis=2)
        v_own = lax.dynamic_slice_in_dim(v, blk * BS, BS, axis=2)
        s_own = jnp.einsum('bhqd,bhkd->bhqk', qb, k_own).astype(jnp.float32) * scale
        causal = (blk * BS + jnp.arange(BS))[None, :] <= (start + jnp.arange(QC))[:, None]
        s_own = jnp.where(causal, s_own, -jnp.inf)
        p = jax.nn.softmax(jnp.concatenate([s_own, s_sel], axis=-1), axis=-1).astype(v.dtype)
        p_own = p[..., :BS]
        p_sel = p[..., BS:].reshape(B, H, QC, topk, BS)
        return (jnp.einsum('bhqk,bhkd->bhqd', p_own, v_own)
                + jnp.einsum('bhqjk,bhqjkd->bhqd', p_sel, v_sel))

    o = lax.map(one, jnp.arange(Sp // QC))
    o = jnp.moveaxis(o, 0, 2).reshape(B, H, Sp, D)[:, :, :S]
    return o.transpose(0, 2, 1, 3).reshape(B, S, H * D)


def rglru_mixer(xb, gate, conv_w, conv_b, w_a, b_a, w_x, b_x, lam):
    B, S, W = xb.shape
    xc = lax.conv_general_dilated(xb, conv_w, window_strides=(1,), padding=[(LRU_CONV - 1, 0)],
                                  dimension_numbers=('NWC', 'WIO', 'NWC'),
                                  feature_group_count=W) + conv_b
    xh = xc.reshape(B, S, LRU_BLOCKS, LRU_BW)
    r = jax.nn.sigmoid(jnp.einsum('bsnc,ncd->bsnd', xh, w_a).reshape(B, S, W) + b_a)
    i = jax.nn.sigmoid(jnp.einsum('bsnc,ncd->bsnd', xh, w_x).reshape(B, S, W) + b_x)
    log_a = (-LRU_C * r.astype(jnp.float32)) * jax.nn.softplus(-lam.astype(jnp.float32))
    a = jnp.exp(log_a)
    u = jnp.sqrt(-jnp.expm1(2.0 * log_a)) * (i * xc).astype(jnp.float32)

    def combine(left, right):
        a_l, b_l = left
        a_r, b_r = right
        return a_l * a_r, a_r * b_l + b_r

    _, h = lax.associative_scan(combine, (a, u), axis=1)
    return (h * jax.nn.gelu(gate.astype(jnp.float32))).astype(xb.dtype)


def setup_inputs(seed: int = 0) -> dict:
    key = jax.random.key(seed)
    keys = iter(jax.random.split(key, 32))
    L = DEPTH

    def dense(shape, fan_in):
        return jax.random.normal(next(keys), shape, jnp.float32) * fan_in ** -0.5

    def gain(shape):
        return 1.0 + 0.02 * jax.random.normal(next(keys), shape, jnp.float32)

    def bias(shape):
        return 0.02 * jax.random.normal(next(keys), shape, jnp.float32)

    x = jax.random.normal(next(keys), (BATCH, SEQ, D_MODEL), jnp.float32)
    attn_norm = gain((L, D_MODEL))
    w_in = dense((L, D_MODEL, IN_WIDTH), D_MODEL)
    mla_q_norm = gain((L, MLA_Q_RANK))
    mla_w_uq = dense((L, MLA_Q_RANK, MLA_HEADS * (MLA_NOPE + MLA_ROPE)), MLA_Q_RANK)
    mla_kv_norm = gain((L, MLA_KV_RANK))
    mla_w_ukv = dense((L, MLA_KV_RANK, MLA_HEADS * (MLA_NOPE + MLA_V)), MLA_KV_RANK)
    gla_w_a2 = dense((L, GLA_GATE_RANK, GLA_HEADS * GLA_DK), GLA_GATE_RANK)
    gla_b_a2 = bias((L, GLA_HEADS * GLA_DK))
    gla_head_norm = gain((L, GLA_DV))
    lru_conv_w = dense((L, LRU_CONV, 1, LRU_WIDTH), LRU_CONV)
    lru_conv_b = bias((L, LRU_WIDTH))
    lru_w_a = dense((L, LRU_BLOCKS, LRU_BW, LRU_BW), LRU_BW)
    lru_b_a = bias((L, LRU_WIDTH))
    lru_w_x = dense((L, LRU_BLOCKS, LRU_BW, LRU_BW), LRU_BW)
    lru_b_x = bias((L, LRU_WIDTH))
    u = jax.random.uniform(next(keys), (L, LRU_WIDTH), jnp.float32, 0.9, 0.999)
    a0 = u ** (1.0 / LRU_C)
    lru_lambda = jnp.log(a0) - jnp.log1p(-a0)
    w_out = dense((L, MIX_WIDTH, D_MODEL), MIX_WIDTH)
    ffn_norm = gain((L, D_MODEL))
    w_ffn_in = dense((L, D_MODEL, 2 * FFN_HIDDEN), D_MODEL)
    w_ffn_out = dense((L, FFN_HIDDEN, D_MODEL), FFN_HIDDEN)
    final_norm = gain((D_MODEL,))
    return {
        "x": x, "attn_norm": attn_norm, "w_in": w_in,
        "mla_q_norm": mla_q_norm, "mla_w_uq": mla_w_uq, "mla_kv_norm": mla_kv_norm, "mla_w_ukv": mla_w_ukv,
        "gla_w_a2": gla_w_a2, "gla_b_a2": gla_b_a2, "gla_head_norm": gla_head_norm,
        "lru_conv_w": lru_conv_w, "lru_conv_b": lru_conv_b, "lru_w_a": lru_w_a, "lru_b_a": lru_b_a,
        "lru_w_x": lru_w_x, "lru_b_x": lru_b_x, "lru_lambda": lru_lambda,
        "w_out": w_out, "ffn_norm": ffn_norm, "w_ffn_in": w_ffn_in, "w_ffn_out": w_ffn_out,
        "final_norm": final_norm,
    }


def reference(x, attn_norm, w_in, mla_q_norm, mla_w_uq, mla_kv_norm, mla_w_ukv,
              gla_w_a2, gla_b_a2, gla_head_norm,
              lru_conv_w, lru_conv_b, lru_w_a, lru_b_a, lru_w_x, lru_b_x, lru_lambda,
              w_out, ffn_norm, w_ffn_in, w_ffn_out, final_norm):
    S = x.shape[1]
    cos_mla, sin_mla = rope_tables(S, MLA_ROPE)
    cos_moba, sin_moba = rope_tables(S, MOBA_ROT)
    for l in range(DEPTH):
        h = rms_norm(x, attn_norm[l])
        (mq, mkv, mkr, gq, gk, gv, ga, gg, bq, bk, bv, rx, rg) = jnp.split(h @ w_in[l], SPLIT_POINTS, axis=-1)
        y_a = mla_mixer(mq, mkv, mkr, mla_q_norm[l], mla_w_uq[l], mla_kv_norm[l], mla_w_ukv[l], cos_mla, sin_mla)
        y_b = gla_mixer(gq, gk, gv, ga, gg, gla_w_a2[l], gla_b_a2[l], gla_head_norm[l])
        y_c = moba_mixer(bq, bk, bv, cos_moba, sin_moba)
        y_d = rglru_mixer(rx, rg, lru_conv_w[l], lru_conv_b[l], lru_w_a[l], lru_b_a[l],
                          lru_w_x[l], lru_b_x[l], lru_lambda[l])
        y = jnp.concatenate([y_a.astype(x.dtype), y_b.astype(x.dtype), y_c.astype(x.dtype), y_d.astype(x.dtype)], axis=-1)
        x = x + y @ w_out[l]
        h = rms_norm(x, ffn_norm[l])
        g, up = jnp.split(h @ w_ffn_in[l], 2, axis=-1)
        x = x + (jax.nn.silu(g) * up) @ w_ffn_out[l]
    return rms_norm(x, final_norm)
```

```python
import numpy as np
import ml_dtypes
import concourse.bass as bass
import concourse.mybir as mybir
from concourse.bass_utils import run_bass_kernel_spmd

F32 = mybir.dt.float32
BF16 = mybir.dt.bfloat16
AF = mybir.ActivationFunctionType
ALU = mybir.AluOpType
AX = mybir.AxisListType

ENGS = ("tensor", "vector", "scalar", "gpsimd", "sync")
import os as _os
STRICT_SAME_ENGINE = bool(int(_os.environ.get("STRICT", "1")))

DBG_SKIP_GLA = bool(int(_os.environ.get('DBG_SKIP_GLA', '0')))
DBG_GLA_STAGE = int(_os.environ.get('DBG_GLA_STAGE', '99'))


class Buf:
    def __init__(self, name, t=None, excl=False):
        self.name = name
        self.t = t
        self.excl = excl
        self.writer = []
        self.readers = []

    def __getitem__(self, idx):
        return self.t[idx]


class Prog:
    def __init__(self, nc, dma_ring=24):
        self.nc = nc
        self.q = {e: [] for e in ENGS}
        self.sem = {e: nc.alloc_semaphore(name=f"s_{e}") for e in ENGS}
        self.cnt = {e: 0 for e in ENGS}
        self.waited = {e: {} for e in ENGS}
        self.ring = {}
        for e in ("sync", "gpsimd", "scalar"):
            n = dma_ring if e != "scalar" else 8
            self.ring[e] = {"sems": [nc.alloc_semaphore(name=f"d_{e}{i}") for i in range(n)],
                            "val": [0] * n, "pos": 0}
        self.ninst = 0
        self.ccsem = nc.alloc_semaphore(name="s_cc")
        self.cccnt = 0
        self.roff_val = None

    def raw(self, eng, fn):
        self.q[eng].append(fn)

    def _deps(self, reads, writes, extra):
        deps = list(extra)
        for b in reads:
            deps += b.writer
        for b in writes:
            deps += b.writer
            deps += b.readers
        return deps

    def _commit(self, tok, reads, writes):
        for b in reads:
            b.readers.append(tok)
            if len(b.readers) > 64:
                b.readers = b.readers[-64:]
        for b in writes:
            b.writer = [tok]
            b.readers = []

    def _waits(self, eng, deps):
        out = []
        w = self.waited[eng]
        best = {}
        for (s, v, e_src) in deps:
            if e_src == eng and (eng == "tensor" or not STRICT_SAME_ENGINE):
                continue
            k = id(s)
            if w.get(k, 0) >= v:
                continue
            if k not in best or best[k][1] < v:
                best[k] = (s, v)
        for k, (s, v) in best.items():
            w[k] = v
            out.append((s, v))
        return out

    def op(self, eng, fn, reads=(), writes=(), deps=(), signal=True):
        if any(b.excl for b in reads):
            writes = list(writes) + [b for b in reads if b.excl and b not in writes]
            reads = [b for b in reads if not b.excl]
        d = self._deps(reads, writes, deps)
        waits = self._waits(eng, d)
        if signal:
            self.cnt[eng] += 1
            tok = (self.sem[eng], self.cnt[eng], eng)
        else:
            tok = None
        sem = self.sem[eng]

        def thunk(e, fn=fn, waits=waits, signal=signal, sem=sem):
            for (s, v) in waits:
                e.wait_ge(s, v)
            ins = fn(e)
            if signal:
                ins.then_inc(sem, 1)
        self.q[eng].append(thunk)
        self.ninst += 1
        if tok is not None:
            self._commit(tok, reads, writes)
        return tok

    def I(self, eng, name, reads, writes, *args, signal=True, **kw):
        return self.op(eng, lambda e, name=name, args=args, kw=kw: getattr(e, name)(*args, **kw),
                       reads=reads, writes=writes, signal=signal)

    def dma(self, eng, out, in_, reads=(), writes=(), deps=(), **kw):
        r = self.ring[eng]
        i = r["pos"]
        r["pos"] = (i + 1) % len(r["sems"])
        s = r["sems"][i]
        d = self._deps(reads, writes, deps)
        if r["val"][i] > 0:
            d.append((s, r["val"][i], "dma"))
        waits = self._waits(eng, d)
        r["val"][i] += 16
        tok = (s, r["val"][i], "dma")

        def thunk(e, waits=waits, s=s, out=out, in_=in_, kw=kw):
            for (ss, v) in waits:
                e.wait_ge(ss, v)
            src = in_(e) if callable(in_) else in_
            e.dma_start(out=out, in_=src, **kw).then_inc(s, 16)
        self.q[eng].append(thunk)
        self.ninst += 1
        self._commit(tok, reads, writes)
        return tok

    def wait_all(self, eng, toks):
        waits = self._waits(eng, list(toks))

        def thunk(e, waits=waits):
            for (s, v) in waits:
                e.wait_ge(s, v)
        self.q[eng].append(thunk)

    def flush(self):
        nc = self.nc
        qs = self.q
        self.q = {e: [] for e in ENGS}
        with nc.Block() as block:
            for e in ENGS:
                if not qs[e]:
                    continue

                def body(engine, lst=qs[e]):
                    for th in lst:
                        th(engine)
                getattr(block, e)(body)


D = 1024
INW = 2480
FH = 2816
EPS = 1e-6
O_MQ, O_MKV, O_MKR = 0, 256, 384
O_GQ, O_GK, O_GV, O_GA, O_GG = 416, 544, 672, 928, 944
O_BQ, O_BK, O_BV = 1200, 1456, 1712
O_RX, O_RG = 1968, 2224
NEGBIG = -30000.0


def vw(t, p0, np_, off, *dims):
    F = 1
    for s in t.shape[1:]:
        F *= s
    return bass.AP(t, p0 * F + off, [[F, np_]] + [[a, b] for (a, b) in dims])


class Ring:
    def __init__(self, items):
        self.items = list(items)
        self.i = 0

    def next(self):
        x = self.items[self.i % len(self.items)]
        self.i += 1
        return x


class Ctx:
    n = 0

    def __init__(self, nc, P):
        from contextlib import ExitStack
        self.nc, self.P = nc, P
        self.st = ExitStack()

    def sb(self, name, shape, dt):
        Ctx.n += 1
        return Buf(name, self.st.enter_context(self.nc.sbuf_tensor(f"{name}_{self.n}", list(shape), dt)))

    def ps(self, name, dt=F32):
        Ctx.n += 1
        shape = [128, 512] if dt == F32 else [128, 1024]
        return Buf(name, self.st.enter_context(self.nc.psum_tensor(f"{name}_{self.n}", shape, dt)), excl=True)

    def close(self):
        self.st.close()


def mm_group(P, bank, mms, reads):
    n = len(mms)
    for i, (o, l, r) in enumerate(mms):
        first, last = (i == 0), (i == n - 1)

        def fn(e, o=o, l=l, r=r, first=first, last=last):
            return e.matmul(o, lhsT=l, rhs=r, start=first, stop=last)
        P.op("tensor", fn, reads=reads if (first or last) else (), writes=[bank] if (first or last) else (),
             signal=last)


def mm1(P, bank, o, l, r, reads, start=True, stop=True, skip=False):
    def fn(e):
        return e.matmul(o, lhsT=l, rhs=r, start=start, stop=stop, skip_group_check=skip)
    P.op("tensor", fn, reads=reads, writes=[bank])


def drain_dma(P):
    toks = []
    for e, r in P.ring.items():
        toks += [(s, v, "dma") for s, v in zip(r["sems"], r["val"]) if v > 0]
    P.wait_all("sync", toks)


def rms_rstd(P, ss, rt, rstd, n_scale, bufs_r=()):
    P.op("scalar", lambda e: e.activation(out=rt[:], in_=ss[:], func=AF.Sqrt, scale=n_scale, bias=EPS),
         reads=[ss], writes=[rt])
    P.op("vector", lambda e: e.reciprocal(rstd[:], rt[:]), reads=[rt], writes=[rstd])


def mm_group(P, bank, mms, reads):
    n = len(mms)
    for i, (o, l, r) in enumerate(mms):
        first, last = (i == 0), (i == n - 1)
        fl = first or last
        P.I("tensor", "matmul", reads if fl else (), [bank] if fl else (), o, lhsT=l, rhs=r, start=first, stop=last,
            signal=last)


def mm1(P, bank, o, l, r, reads, start=True, stop=True, skip=False):
    P.I("tensor", "matmul", reads, [bank], o, lhsT=l, rhs=r, start=start, stop=stop, skip_group_check=skip)


def run_chains(P, gens, what):
    alive = list(gens)
    idle = 0
    while alive:
        n0 = P.ninst
        for g in list(alive):
            try:
                r = next(g)
                if r is not None:
                    alive.append(r)
            except StopIteration:
                alive.remove(g)
        idle = idle + 1 if P.ninst == n0 else 0
        assert idle < 1000, f"{what}: scheduler deadlock"


def run_chains(P, gens, what):
    alive = list(gens)
    idle = 0
    while alive:
        n0 = P.ninst
        for g in list(alive):
            try:
                r = next(g)
                if r is not None:
                    alive.append(r)
            except StopIteration:
                alive.remove(g)
        idle = idle + 1 if P.ninst == n0 else 0
        assert idle < 1000, f"{what}: scheduler deadlock"


def transposes(P, pt, items, reads):
    n = len(items)
    for i, (o, a) in enumerate(items):
        fl = (i == 0) or (i == n - 1)
        P.I("tensor", "transpose", reads if fl else (), [pt] if fl else (), o, a, P.ident[0:a.shape[0], 0:a.shape[0]],
            signal=(i == n - 1))


def rms_rstd(P, ss, rt, rstd, n_scale):
    P.I("scalar", "activation", [ss], [rt], out=rt[:], in_=ss[:], func=AF.Sqrt, scale=n_scale, bias=EPS)
    P.I("vector", "reciprocal", [rt], [rstd], rstd[:], rt[:])


def phase_c(nc, P, S, l, dr, ntok, last, dyn=False):
    C = Ctx(nc, P)
    TT = 256
    nt = ntok // TT
    wout = C.sb("wout", [128, 8, 1024], BF16)
    wfi = C.sb("wfi", [128, 8, 2 * FH], BF16)
    wfo = C.sb("wfo", [128, 22, 1024], BF16)
    gF = C.sb("gF", [128, 1024], F32)
    gL = C.sb("gL", [128, 1024], F32) if last else None
    ident = C.sb("ident", [128, 128], BF16)
    P.ident = ident
    xs = [C.sb(f"xs{i}", [128, 2, 1024], F32) for i in range(2)]
    yT = [C.sb(f"yT{i}", [128, 8, TT], BF16) for i in range(2)]
    h2 = C.sb("h2", [128, 1024], BF16)
    h2Ts = [C.sb(f"h2T{i}", [128, 8, TT], BF16) for i in range(2)]
    actT = C.sb("actT", [128, 22, TT], BF16)
    sgr = Ring([C.sb(f"sg{i}", [128, TT], F32) for i in range(2)])
    junk = C.sb("junk", [128, 1024], BF16)
    ss = C.sb("ss", [128, 2], F32)
    rt = C.sb("rt", [128, 2], F32)
    rstd = C.sb("rstd", [128, 2], F32)
    ss2 = C.sb("ss2", [128, 2], F32)
    rt2 = C.sb("rt2", [128, 2], F32)
    rstd2 = C.sb("rstd2", [128, 2], F32)
    psf = Ring([C.ps(f"pf{i}") for i in range(6)])
    psb = Ring([C.ps(f"pb{i}", BF16) for i in range(2)])

    w_out, w_fi, w_fo = dr["w_out"], dr["w_ffn_in"], dr["w_ffn_out"]
    P.dma("sync", ident[:], dr["c_ident"].ap(), writes=[ident])
    P.dma("sync", gF[:], bass.AP(dr["ffn_norm"], l * D, [[0, 128], [1, D]]), writes=[gF])
    if last:
        P.dma("sync", gL[:], bass.AP(dr["final_norm"], 0, [[0, 128], [1, D]]), writes=[gL])

    if dyn:
        roff_sb = C.sb("roff_sb", [1, 1], mybir.dt.int32)
        tokd = P.dma("sync", roff_sb[:], dr["roff"].ap(), writes=[roff_sb])

        def setreg(e, tokd=tokd):
            e.wait_ge(tokd[0], tokd[1])
            e.reg_load(P.sync_reg, roff_sb[:1, :1])
            P.roff_val = e.snap(P.sync_reg)
        P.raw("sync", setreg)
    yTv = dr["yT"].ap().rearrange("(c p) s -> p c s", p=128)

    def load(u):
        b = u % 2
        r0 = u * TT
        P.dma("sync", xs[b][:], bass.AP(dr["xc_in"], r0 * D, [[D, 128], [128 * D, 2], [1, D]]), writes=[xs[b]])
        if dyn:
            P.dma("sync", yT[b][:], lambda e, r0=r0: yTv[:, :, bass.ds(P.roff_val + r0, TT)], writes=[yT[b]])
        else:
            P.dma("sync", yT[b][:], bass.AP(dr["yT"], r0, [[S, 128], [128 * S, 8], [1, TT]]), writes=[yT[b]])

    for c in range(8):
        P.dma("gpsimd", wout[:, c, :], bass.AP(w_out, l * D * D + c * 128 * D, [[D, 128], [1, D]]), writes=[wout])
    for c in range(8):
        P.dma("gpsimd", wfi[:, c, :], bass.AP(w_fi, l * D * 2 * FH + c * 128 * 2 * FH, [[2 * FH, 128], [1, 2 * FH]]),
              writes=[wfi])
    for j in range(22):
        P.dma("gpsimd", wfo[:, j, :], bass.AP(w_fo, l * FH * D + j * 128 * D, [[D, 128], [1, D]]), writes=[wfo])

    class BankPool:
        def __init__(self, banks):
            self.free = list(banks)

        def get(self):
            while not self.free:
                yield
            return self.free.pop(0)

        def put(self, b_):
            self.free.append(b_)

    FP = BankPool(psf.items)
    BP = BankPool(psb.items)
    front_done = [0]
    ffn_done = [0]

    def front_chain():
        for u in range(nt):
            b = u % 2
            while loaded[0] <= u:
                yield
            if u >= 2:
                for _ in range(16):
                    yield
            X, Y, H2T = xs[b], yT[b], h2Ts[b]
            for s_ in range(2):
                for n in range(2):
                    pb = yield from FP.get()
                    mm_group(P, pb, [(pb[:, :], Y[:, c, s_ * 128:(s_ + 1) * 128], wout[:, c, n * 512:(n + 1) * 512])
                                     for c in range(8)], reads=[Y, wout])
                    xsl = X[:, s_, n * 512:(n + 1) * 512]
                    P.I("vector", "tensor_tensor", [pb, X], [X], out=xsl, in0=pb[:, :], in1=xsl, op=ALU.add)
                    FP.put(pb)
                    yield
            for s_ in range(2):
                P.I("scalar", "activation", [X], [junk, ss], out=junk[:], in_=X[:, s_, :], func=AF.Square,
                    scale=float(D ** -0.5), accum_out=ss[:, s_:s_ + 1])
                yield
            rms_rstd(P, ss, rt, rstd, 1.0)
            yield
            for s_ in range(2):
                P.I("vector", "scalar_tensor_tensor", [X, rstd, gF], [h2], out=h2[:], in0=X[:, s_, :],
                    scalar=rstd[:, s_:s_ + 1], in1=gF[:], op0=ALU.mult, op1=ALU.mult)
                yield
                pt = yield from BP.get()
                transposes(P, pt, [(pt[:, c * 128:(c + 1) * 128], h2[:, c * 128:(c + 1) * 128]) for c in range(8)], [h2, ident])
                P.I("scalar", "copy", [pt], [H2T], out=H2T[:, :, s_ * 128:(s_ + 1) * 128],
                    in_=vw(pt.t, 0, 128, 0, (128, 8), (1, 128)))
                BP.put(pt)
                yield
            front_done[0] = u + 1
            yield

    loaded = [0]

    def load_chain():
        for u in range(nt):
            while u >= 2 and ffn_done[0] < u - 1:
                yield
            load(u)
            loaded[0] = u + 1
            yield

    def ffn_chain():
        for u in range(nt):
            b = u % 2
            while front_done[0] <= u:
                yield
            X, H2T = xs[b], h2Ts[b]
            for j in range(22):
                pb = yield from FP.get()
                mm_group(P, pb, [(pb[:, 0:TT], wfi[:, c, j * 128:(j + 1) * 128], H2T[:, c, :]) for c in range(8)],
                         reads=[wfi, H2T])
                yield
                mm_group(P, pb, [(pb[:, TT:2 * TT], wfi[:, c, FH + j * 128:FH + (j + 1) * 128], H2T[:, c, :])
                                 for c in range(8)], reads=[wfi, H2T])
                g_ = sgr.next()
                P.I("scalar", "activation", [pb], [g_], out=g_[:], in_=pb[:, 0:TT], func=AF.Silu)
                P.I("vector", "tensor_tensor", [pb, g_], [actT], out=actT[:, j, :], in0=pb[:, TT:2 * TT], in1=g_[:], op=ALU.mult)
                FP.put(pb)
                yield
            for s_ in range(2):
                for n in range(2):
                    pb = yield from FP.get()
                    mm_group(P, pb, [(pb[:, :], actT[:, j, s_ * 128:(s_ + 1) * 128], wfo[:, j, n * 512:(n + 1) * 512])
                                     for j in range(22)], reads=[actT, wfo])
                    xsl = X[:, s_, n * 512:(n + 1) * 512]
                    P.I("vector", "tensor_tensor", [pb, X], [X], out=xsl, in0=pb[:, :], in1=xsl, op=ALU.add)
                    FP.put(pb)
                    yield
            if last:
                for s_ in range(2):
                    P.I("scalar", "activation", [X], [junk, ss2], out=junk[:], in_=X[:, s_, :], func=AF.Square,
                        scale=float(D ** -0.5), accum_out=ss2[:, s_:s_ + 1])
                rms_rstd(P, ss2, rt2, rstd2, 1.0)
                yield
                for s_ in range(2):
                    P.I("vector", "scalar_tensor_tensor", [X, rstd2, gL], [X], out=X[:, s_, :], in0=X[:, s_, :],
                        scalar=rstd2[:, s_:s_ + 1], in1=gL[:], op0=ALU.mult, op1=ALU.mult)
                yield
            r0 = u * TT
            P.dma("gpsimd", bass.AP(dr["xout"], r0 * D, [[D, 128], [128 * D, 2], [1, D]]), X[:], reads=[X])
            ffn_done[0] = u + 1
            yield

    run_chains(P, [load_chain(), front_chain(), ffn_chain()], "phase C")
    drain_dma(P)
    P.flush()
    C.close()


def phase_a(nc, P, S, l, dr, NH=4):
    roles = (0, 1)
    C = Ctx(nc, P)
    NT = S // 128
    NJ = NH // 2
    NMLA, NMOBA = 2, 2
    F = NH * 32
    V4 = NH * 64
    O_MQ, O_MKV, O_MKR = 0, 256, 384
    O_GQ = 416
    O_GK = O_GQ + F
    O_GV = O_GK + F
    O_GA = O_GV + V4
    O_GG = O_GA + 16
    O_BQ = O_GG + V4
    O_BK = O_BQ + V4
    O_BV = O_BK + V4
    O_RX = O_BV + V4
    O_RG = O_RX + NJ * 128
    INW = O_RG + NJ * 128
    I = P.I
    ident = C.sb("ident", [128, 128], BF16)
    P.ident = ident
    win = C.sb("win", [128, 8, INW], BF16)
    gA = C.sb("gA", [128, 1024], F32)
    xs = [C.sb(f"xs{i}", [128, 1024], F32) for i in range(2)]
    h = C.sb("h", [128, 1024], BF16)
    hT = [C.sb(f"hT{i}", [128, 8, 128], BF16) for i in range(3)]
    junk = C.sb("junk", [128, 1024], BF16)
    ss = C.sb("ss", [128, 4], F32)
    rt = C.sb("rt", [128, 4], F32)
    rstd = C.sb("rstd", [128, 4], F32)
    psf = Ring([C.ps(f"pf{i}") for i in range(5)])
    psb = Ring([C.ps(f"pb{i}", BF16) for i in range(3)])
    evac = Ring(["scalar", "vector"])

    taps = dr.setdefault("_taps", []) if "dbg" in dr else None

    def tap(name, buf, ap, np_, ncol):
        if taps is None:
            return
        i = len(taps)
        taps.append((name, np_, ncol))
        P.dma("gpsimd", bass.AP(dr["dbg"], i * 128 * 512, [[512, np_], [1, ncol]]), ap, reads=[buf])

    def copy(eng, reads, writes, out, in_):
        if eng == "scalar":
            I("scalar", "copy", reads, writes, out=out, in_=in_)
        else:
            I(eng, "tensor_copy", reads, writes, out, in_)

    P.dma("sync", ident[:], dr["c_ident"].ap(), writes=[ident])
    P.dma("sync", gA[:], bass.AP(dr["attn_norm"], l * D, [[0, 128], [1, D]]), writes=[gA])

    xrow = dr.get("xin_rowmap") or (lambda t0: t0)

    def load(t):
        P.dma("sync", xs[t % 2][:], bass.AP(dr["xin"], xrow(t * 128) * D, [[D, 128], [1, D]]), writes=[xs[t % 2]])

    load(0)
    for c in range(8):
        P.dma("gpsimd", win[:, c, :], bass.AP(dr["w_in"], l * D * INW + c * 128 * INW, [[INW, 128], [1, INW]]), writes=[win])

    if 0 in roles:
        wuq = C.sb("wuq", [128, 2, NH * 96], BF16)
        wukv = C.sb("wukv", [128, NH * 128], BF16)
        gq = C.sb("gq", [128, 256], F32)
        gkv = C.sb("gkv", [128, 128], F32)
        cosA = C.sb("cosA", [128, NT, 16], F32)
        sinA = C.sb("sinA", [128, NT, 16], F32)
        MB = []
        for inst_ in range(NMLA):
            cqn = C.sb("cqn", [128, 256], BF16)
            ckvn = C.sb("ckvn", [128, 128], BF16)
            cqnT = C.sb("cqnT", [128, 2, 128], BF16)
            ckvnT = C.sb("ckvnT", [128, 128], BF16)
            kpe = C.sb("kpe", [128, 32], F32)
            tmp = [C.sb(f"tmp{i}", [128, NH * 16], F32) for i in range(4)]
            qtm = C.sb("qtm", [128, NH, 96], BF16)
            ktm = C.sb("ktm", [128, NH, 96], BF16)
            vtm = [C.sb(f"vtm{i}", [128, NH, 65], BF16) for i in range(2)]
            qTs = [C.sb(f"qTs{i}", [96, NH, 128], BF16) for i in range(2)]
            kTs = [C.sb(f"kTs{i}", [96, NH, 128], BF16) for i in range(2)]
            junk2 = C.sb("junk2", [128, 256], BF16)
            ss2 = C.sb("ss2", [128, 4], F32)
            rt2 = C.sb("rt2", [128, 4], F32)
            rstd2 = C.sb("rstd2", [128, 4], F32)
            for v_ in vtm:
                I("gpsimd", "memset", [], [v_], v_[:], 1.0)
            MB.append(dict(cqn=cqn, ckvn=ckvn, cqnT=cqnT, ckvnT=ckvnT, kpe=kpe, tmp=tmp, qtm=qtm, ktm=ktm, vtm=vtm, qTs=qTs, kTs=kTs, junk2=junk2, ss2=ss2, rt2=rt2, rstd2=rstd2))
        for c in range(2):
            P.dma("gpsimd", wuq[:, c, :], bass.AP(dr["mla_w_uq"], l * 256 * NH * 96 + c * 128 * NH * 96, [[NH * 96, 128], [1, NH * 96]]), writes=[wuq])
        P.dma("gpsimd", wukv[:], bass.AP(dr["mla_w_ukv"], l * 128 * NH * 128, [[NH * 128, 128], [1, NH * 128]]), writes=[wukv])
        P.dma("sync", gq[:], bass.AP(dr["mla_q_norm"], l * 256, [[0, 128], [1, 256]]), writes=[gq])
        P.dma("sync", gkv[:], bass.AP(dr["mla_kv_norm"], l * 128, [[0, 128], [1, 128]]), writes=[gkv])
        P.dma("sync", cosA[:], bass.AP(dr["c_cosA"], 0, [[16, 128], [128 * 16, NT], [1, 16]]), writes=[cosA])
        P.dma("sync", sinA[:], bass.AP(dr["c_sinA"], 0, [[16, 128], [128 * 16, NT], [1, 16]]), writes=[sinA])
        wa2 = C.sb("wa2", [17, F], BF16)
        ghn = C.sb("ghn", [128, 64], F32)
        cum = C.sb("cum", [128, 128], F32)
        chones = C.sb("chones", [128, 128], F32)
        chind = C.sb("chind", [128, 2], F32)
        gmask = C.sb("gmask", [128, 128], BF16)
        bdm = C.sb("bdm", [128, 256], F32)
        Sf = Ring([C.sb(f"Sf{i}", [128, V4], F32) for i in range(6)])
        Sb = Ring([C.sb(f"Sb{i}", [128, V4], BF16) for i in range(6)])
        GB = []
        for inst_ in range(2):
            alow = C.sb("alow", [128, 16], BF16)
            alT = [C.sb(f"alT{i}", [17, 128], BF16) for i in range(2)]
            e1 = C.sb("e1", [128, F], F32)
            sp = C.sb("sp", [128, F], F32)
            cbs = C.sb("cbs", [128, F], F32)
            d1 = C.sb("d1", [128, F], F32)
            d2 = C.sb("d2", [128, F], F32)
            E = [C.sb(f"E{i}", [128, F], F32) for i in range(4)]
            dec = C.sb("dec", [F, 2], F32)
            qt_ = C.sb("qt_", [128, F], BF16)
            qh_ = C.sb("qh_", [128, F], BF16)
            kt_ = C.sb("kt_", [128, F], BF16)
            kh_ = C.sb("kh_", [128, F], BF16)
            vbf = C.sb("vbf", [128, V4], BF16)
            qtT = C.sb("qtT", [32, NH, 128], BF16)
            ktT = C.sb("ktT", [32, NH, 128], BF16)
            qhT = C.sb("qhT", [F, 128], BF16)
            attm = C.sb("attm", [128, NH, 128], BF16)
            tmpS = C.sb("tmpS", [128, V4], F32)
            sq = C.sb("sq", [128, V4], F32)
            ss4 = C.sb("ss4", [128, NH], F32)
            rt4 = C.sb("rt4", [128, NH], F32)
            rstd4 = C.sb("rstd4", [128, NH], F32)
            on = C.sb("on", [128, V4], F32)
            on2 = C.sb("on2", [128, V4], F32)
            sgl = C.sb("sgl", [128, V4], F32)
            yb = C.sb("yb", [128, V4], BF16)
            ybT = [C.sb(f"ybT{i}", [128, NJ, 128], BF16) for i in range(2)]
            GB.append(dict(alow=alow, alT=alT, e1=e1, sp=sp, cbs=cbs, d1=d1, d2=d2, E=E, dec=dec, qt_=qt_, qh_=qh_, kt_=kt_, kh_=kh_, vbf=vbf, qtT=qtT, ktT=ktT, qhT=qhT, attm=attm, tmpS=tmpS, sq=sq, ss4=ss4, rt4=rt4, rstd4=rstd4, on=on, on2=on2, sgl=sgl, yb=yb, ybT=ybT))
        P.dma("gpsimd", wa2[0:16, :], bass.AP(dr["gla_w_a2"], l * 16 * F, [[F, 16], [1, F]]), writes=[wa2])
        P.dma("gpsimd", wa2[16:17, :], bass.AP(dr["gla_b_a2"], l * F, [[F, 1], [1, F]]), writes=[wa2])
        P.dma("sync", ghn[:], bass.AP(dr["gla_head_norm"], l * 64, [[0, 128], [1, 64]]), writes=[ghn])
        P.dma("sync", cum[:], dr["c_cum"].ap(), writes=[cum])
        P.dma("sync", chones[:], dr["c_chones"].ap(), writes=[chones])
        P.dma("sync", chind[:], dr["c_chind"].ap(), writes=[chind])
        P.dma("sync", gmask[:], dr["c_gmask"].ap(), writes=[gmask])
        P.dma("sync", bdm[:], dr["c_bdm"].ap(), writes=[bdm])
        for G_ in GB:
            for a_ in G_["alT"]:
                I("gpsimd", "memset", [], [a_], a_[:], 1.0)
        Sa_f, Sa_b = Sf.next(), Sb.next()
        I("gpsimd", "memset", [], [Sa_f], Sa_f[:], 0.0)
        I("gpsimd", "memset", [], [Sa_b], Sa_b[:], 0.0)

    if 1 in roles:
        cosC = C.sb("cosC", [128, NT, 8], F32)
        sinC = C.sb("sinC", [128, NT, 8], F32)
        OB = []
        for inst_ in range(NMOBA):
            qktm = C.sb("qktm", [128, 2 * NH, 64], BF16)
            rtmp = [C.sb(f"rtmp{i}", [128, 2 * NH, 8], F32) for i in range(4)]
            vtc = [C.sb(f"vtc{i}", [128, NH, 65], BF16) for i in range(2)]
            qkTs = [C.sb(f"qkTs{i}", [64, 2 * NH, 128], BF16) for i in range(2)]
            for v_ in vtc:
                I("gpsimd", "memset", [], [v_], v_[:], 1.0)
            OB.append(dict(qktm=qktm, rtmp=rtmp, vtc=vtc, qkTs=qkTs))
        P.dma("sync", cosC[:], bass.AP(dr["c_cosC"], 0, [[8, 128], [128 * 8, NT], [1, 8]]), writes=[cosC])
        P.dma("sync", sinC[:], bass.AP(dr["c_sinC"], 0, [[8, 128], [128 * 8, NT], [1, 8]]), writes=[sinC])
        cw = C.sb("cw", [128, 4, NJ], F32)
        cbv = C.sb("cbv", [128, NJ], F32)
        bab = C.sb("bab", [128, NJ], F32)
        bxb = C.sb("bxb", [128, NJ], F32)
        lam = C.sb("lam", [128, NJ], F32)
        lt = [C.sb(f"lt{i}", [128, NJ], F32) for i in range(4)]
        cc = C.sb("cc", [128, NJ], F32)
        BDa = C.sb("BDa", [128, NJ, 128], BF16)
        BDx = C.sb("BDx", [128, NJ, 128], BF16)
        xpad = [C.sb(f"xpad{i}", [128, NJ, 131], F32) for i in range(2)]
        xc = C.sb("xc", [128, NJ, 128], F32)
        xcb = C.sb("xcb", [128, NJ, 128], BF16)
        r_ = C.sb("r_", [128, NJ, 128], F32)
        i_ = C.sb("i_", [128, NJ, 128], F32)
        a_ = C.sb("a_", [128, NJ, 128], F32)
        a2 = C.sb("a2", [128, NJ, 128], F32)
        ug = C.sb("ug", [128, NJ, 128], F32)
        m_ = C.sb("m_", [128, NJ, 128], F32)
        u_ = C.sb("u_", [128, NJ, 128], F32)
        hs = [C.sb(f"hs{i}", [128, NJ, 128], F32) for i in range(2)]
        gl = C.sb("gl", [128, NJ, 128], F32)
        yd = [C.sb(f"yd{i}", [128, NJ, 128], BF16) for i in range(2)]
        for k in range(4):
            for j in range(NJ):
                P.dma("sync", cw[:, k, j:j + 1], bass.AP(dr["lru_conv_w"], (l * 4 + k) * NJ * 128 + j * 128, [[1, 128], [1, 1]]),
                      writes=[cw])
        for (tile_, name) in ((cbv, "lru_conv_b"), (bab, "lru_b_a"), (bxb, "lru_b_x"), (lam, "lru_lambda")):
            for j in range(NJ):
                P.dma("sync", tile_[:, j:j + 1], bass.AP(dr[name], l * NJ * 128 + j * 128, [[1, 128], [1, 1]]), writes=[tile_])
        I("gpsimd", "memset", [], [BDa], BDa[:], 0.0)
        I("gpsimd", "memset", [], [BDx], BDx[:], 0.0)
        for (bd, name) in ((BDa, "lru_w_a"), (BDx, "lru_w_x")):
            for n in range(2 * NJ):
                j, q = n // 2, n % 2
                P.dma("gpsimd", bd[q * 64:(q + 1) * 64, j, q * 64:(q + 1) * 64],
                      bass.AP(dr[name], (l * 2 * NJ + n) * 64 * 64, [[64, 64], [1, 64]]), writes=[bd])
        al, ee, s_, acc = lt
        I("vector", "tensor_scalar_mul", [lam], [al], al[:], lam[:], -1.0)
        I("vector", "tensor_tensor", [lam, al], [al], out=al[:], in0=al[:], in1=lam[:], op=ALU.max)
        I("scalar", "activation", [al], [ee], out=ee[:], in_=al[:], func=AF.Exp, scale=-1.0)
        I("vector", "tensor_scalar_add", [ee], [s_], s_[:], ee[:], 2.0)
        I("vector", "reciprocal", [s_], [s_], s_[:], s_[:])
        I("vector", "tensor_tensor", [s_, ee], [s_], out=s_[:], in0=s_[:], in1=ee[:], op=ALU.mult)
        I("vector", "tensor_tensor", [s_], [ee], out=ee[:], in0=s_[:], in1=s_[:], op=ALU.mult)
        I("vector", "memset", [], [acc], acc[:], 1.0 / 17.0)
        for k in (15, 13, 11, 9, 7, 5, 3, 1):
            I("vector", "tensor_tensor", [acc, ee], [acc], out=acc[:], in0=acc[:], in1=ee[:], op=ALU.mult)
            I("vector", "tensor_scalar_add", [acc], [acc], acc[:], acc[:], 1.0 / k)
        I("vector", "tensor_tensor", [acc, s_], [acc], out=acc[:], in0=acc[:], in1=s_[:], op=ALU.mult)
        I("vector", "tensor_tensor", [al, lam], [al], out=al[:], in0=al[:], in1=lam[:], op=ALU.subtract)
        I("vector", "tensor_scalar", [al], [al], out=al[:], in0=al[:], scalar1=-4.0, scalar2=None, op0=ALU.mult)
        I("vector", "scalar_tensor_tensor", [acc, al], [cc], out=cc[:], in0=acc[:], scalar=-16.0, in1=al[:],
          op0=ALU.mult, op1=ALU.add)
        I("gpsimd", "memset", [], [xpad[1]], xpad[1][:], 0.0)
        nbab = C.sb("nbab", [128, NJ], F32)
        nbxb = C.sb("nbxb", [128, NJ], F32)
        I("vector", "tensor_scalar_mul", [bab], [nbab], nbab[:], bab[:], -1.0)
        I("vector", "tensor_scalar_mul", [bxb], [nbxb], nbxb[:], bxb[:], -1.0)

    NHT = 3
    NCH = 4
    ht_ready = [0]
    ht_done = [0] * NT
    st = {}
    if 0 in roles:
        st['Sa_f'], st['Sa_b'] = Sa_f, Sa_b
        st['ready'] = 0

    class BankPool:
        def __init__(self, banks):
            self.free = list(banks)

        def get(self):
            while not self.free:
                yield
            return self.free.pop(0)

        def put(self, b_):
            self.free.append(b_)

    FP = BankPool(psf.items + [])
    BP = BankPool(psb.items + [])

    def proj_tm_(HT, bank, c0, n):
        mm_group(P, bank, [(bank[:, 0:n], HT[:, c, :], win[:, c, c0:c0 + n]) for c in range(8)], [HT, win])

    def norm_chain():
        for t in range(NT):
            b = t % 2
            while t >= NHT and ht_done[t - NHT] < NCH:
                yield
            if t + 1 < NT:
                load(t + 1)
            X = xs[b]
            HT = hT[t % NHT]
            I("scalar", "activation", [X], [junk, ss], out=junk[:], in_=X[:], func=AF.Square, scale=float(D ** -0.5),
              accum_out=ss[:, 0:1])
            yield
            I("scalar", "activation", [ss], [rt], out=rt[:, 0:1], in_=ss[:, 0:1], func=AF.Ln, bias=EPS, scale=1.0)
            yield
            I("scalar", "activation", [rt], [rstd], out=rstd[:, 0:1], in_=rt[:, 0:1], func=AF.Exp, scale=-0.5)
            yield
            I("vector", "scalar_tensor_tensor", [X, rstd, gA], [h], out=h[:], in0=X[:], scalar=rstd[:, 0:1], in1=gA[:],
              op0=ALU.mult, op1=ALU.mult)
            yield
            pt = (yield from BP.get())
            transposes(P, pt, [(pt[:, c * 128:(c + 1) * 128], h[:, c * 128:(c + 1) * 128]) for c in range(8)], [h, ident])
            yield
            copy("scalar", [pt], [HT], HT[:], vw(pt.t, 0, 128, 0, (128, 8), (1, 128)))
            yield
            BP.put(pt)


            ht_ready[0] = t + 1
            yield

    def mla_chain(inst):
        M_ = MB[inst]
        cqn = M_['cqn']
        ckvn = M_['ckvn']
        cqnT = M_['cqnT']
        ckvnT = M_['ckvnT']
        kpe = M_['kpe']
        tmp = M_['tmp']
        qtm = M_['qtm']
        ktm = M_['ktm']
        vtm = M_['vtm']
        qTs = M_['qTs']
        kTs = M_['kTs']
        junk2 = M_['junk2']
        ss2 = M_['ss2']
        rt2 = M_['rt2']
        rstd2 = M_['rstd2']
        for t in range(inst, NT, NMLA):
            b = (t // NMLA) % 2
            while ht_ready[0] <= t:
                yield
            HT = hT[t % NHT]
            pa = (yield from FP.get())
            proj_tm_(HT, pa, O_MQ, 416)
            yield
            ht_done[t] += 1
            I("scalar", "activation", [pa], [junk2, ss2], out=junk2[:, 0:256], in_=pa[:, 0:256], func=AF.Square,
              scale=1.0 / 16.0, accum_out=ss2[:, 1:2])
            yield
            I("scalar", "activation", [pa], [junk2, ss2], out=junk2[:, 0:128], in_=pa[:, 256:384], func=AF.Square,
              scale=float(128 ** -0.5), accum_out=ss2[:, 2:3])
            yield
            I("scalar", "activation", [ss2], [rt2], out=rt2[:, 1:3], in_=ss2[:, 1:3], func=AF.Ln, bias=EPS, scale=1.0)
            yield
            I("scalar", "activation", [rt2], [rstd2], out=rstd2[:, 1:3], in_=rt2[:, 1:3], func=AF.Exp, scale=-0.5)
            yield
            I("vector", "scalar_tensor_tensor", [pa, rstd2, gq], [cqn], out=cqn[:], in0=pa[:, 0:256], scalar=rstd2[:, 1:2],
              in1=gq[:], op0=ALU.mult, op1=ALU.mult)
            yield
            I("vector", "scalar_tensor_tensor", [pa, rstd2, gkv], [ckvn], out=ckvn[:], in0=pa[:, 256:384],
              scalar=rstd2[:, 2:3], in1=gkv[:], op0=ALU.mult, op1=ALU.mult)
            yield
            cs_, sn_ = cosA[:, t, :], sinA[:, t, :]
            I("vector", "tensor_tensor", [pa, cosA], [tmp[0]], out=tmp[0][:, 0:16], in0=pa[:, 384:400], in1=cs_, op=ALU.mult)
            yield
            I("vector", "tensor_tensor", [pa, sinA], [tmp[1]], out=tmp[1][:, 0:16], in0=pa[:, 400:416], in1=sn_, op=ALU.mult)
            yield
            I("vector", "tensor_tensor", [tmp[0], tmp[1]], [kpe], out=kpe[:, 0:16], in0=tmp[0][:, 0:16], in1=tmp[1][:, 0:16], op=ALU.subtract)
            yield
            I("vector", "tensor_tensor", [pa, cosA], [tmp[2]], out=tmp[2][:, 0:16], in0=pa[:, 400:416], in1=cs_, op=ALU.mult)
            yield
            I("vector", "tensor_tensor", [pa, sinA], [tmp[3]], out=tmp[3][:, 0:16], in0=pa[:, 384:400], in1=sn_, op=ALU.mult)
            yield
            I("vector", "tensor_tensor", [tmp[2], tmp[3]], [kpe], out=kpe[:, 16:32], in0=tmp[2][:, 0:16], in1=tmp[3][:, 0:16], op=ALU.add)
            yield
            FP.put(pa)
            I("gpsimd", "tensor_copy", [kpe], [ktm], ktm[:, :, 64:96], vw(kpe.t, 0, 128, 0, (0, NH), (1, 32)))
            yield
            pt = (yield from BP.get())
            transposes(P, pt, [(pt[:, 0:128], cqn[:, 0:128]), (pt[:, 128:256], cqn[:, 128:256]), (pt[:, 256:384], ckvn[:])],
                       [cqn, ckvn, ident])
            yield
            copy("vector", [pt], [cqnT], cqnT[:], vw(pt.t, 0, 128, 0, (128, 2), (1, 128)))
            yield
            copy("scalar", [pt], [ckvnT], ckvnT[:], pt[:, 256:384])
            yield
            BP.put(pt)
            pq = (yield from FP.get())
            mm_group(P, pq, [(pq[:, 0:NH * 96], cqnT[:, c, :], wuq[:, c, :]) for c in range(2)], [cqnT, wuq])
            yield
            pkv = (yield from FP.get())
            mm_group(P, pkv, [(pkv[:, 0:NH * 128], ckvnT[:], wukv[:])], [ckvnT, wukv])
            yield
            copy("scalar", [pq], [qtm], qtm[:, :, 0:64], vw(pq.t, 0, 128, 0, (96, NH), (1, 64)))
            yield
            x1 = vw(pq.t, 0, 128, 64, (96, NH), (1, 16))
            x2 = vw(pq.t, 0, 128, 80, (96, NH), (1, 16))
            cs4 = vw(cosA.t, 0, 128, t * 16, (0, NH), (1, 16))
            sn4 = vw(sinA.t, 0, 128, t * 16, (0, NH), (1, 16))
            tv = [vw(x.t, 0, 128, 0, (16, NH), (1, 16)) for x in tmp]
            I("vector", "tensor_tensor", [pq, cosA], [tmp[0]], out=tv[0], in0=x1, in1=cs4, op=ALU.mult)
            yield
            I("vector", "tensor_tensor", [pq, sinA], [tmp[1]], out=tv[1], in0=x2, in1=sn4, op=ALU.mult)
            yield
            I("vector", "tensor_tensor", [tmp[0], tmp[1]], [qtm], out=qtm[:, :, 64:80], in0=tv[0], in1=tv[1], op=ALU.subtract)
            yield
            I("vector", "tensor_tensor", [pq, cosA], [tmp[2]], out=tv[2], in0=x2, in1=cs4, op=ALU.mult)
            yield
            I("vector", "tensor_tensor", [pq, sinA], [tmp[3]], out=tv[3], in0=x1, in1=sn4, op=ALU.mult)
            yield
            I("vector", "tensor_tensor", [tmp[2], tmp[3]], [qtm], out=qtm[:, :, 80:96], in0=tv[2], in1=tv[3], op=ALU.add)
            yield
            FP.put(pq)
            copy("scalar", [pkv], [ktm], ktm[:, :, 0:64], vw(pkv.t, 0, 128, 0, (128, NH), (1, 64)))
            yield
            VT = vtm[b]
            copy("vector", [pkv], [VT], VT[:, :, 0:64], vw(pkv.t, 0, 128, 64, (128, NH), (1, 64)))
            yield
            FP.put(pkv)
            P.dma("sync", bass.AP(dr["v_a"], t * 128 * NH * 65, [[NH * 65, 128], [1, NH * 65]]), VT[:], reads=[VT])
            yield
            for (src, dstT, dname) in ((qtm, qTs[b], "qT_a"), (ktm, kTs[b], "kT_a")):
                pt = (yield from BP.get())
                transposes(P, pt, [(pt[0:96, hh * 128:(hh + 1) * 128], src[:, hh, :]) for hh in range(NH)], [src, ident])
                yield
                copy(evac.next(), [pt], [dstT], dstT[:], vw(pt.t, 0, 96, 0, (128, NH), (1, 128)))
                yield
                BP.put(pt)
                P.dma("sync", bass.AP(dr[dname], t * 128, [[S, 96], [96 * S, NH], [1, 128]]), dstT[:], reads=[dstT])
                yield


    def gla_chain(inst):
        G_ = GB[inst]
        alow = G_['alow']
        alT = G_['alT']
        e1 = G_['e1']
        sp = G_['sp']
        cbs = G_['cbs']
        d1 = G_['d1']
        d2 = G_['d2']
        E = G_['E']
        dec = G_['dec']
        qt_ = G_['qt_']
        qh_ = G_['qh_']
        kt_ = G_['kt_']
        kh_ = G_['kh_']
        vbf = G_['vbf']
        qtT = G_['qtT']
        ktT = G_['ktT']
        qhT = G_['qhT']
        attm = G_['attm']
        tmpS = G_['tmpS']
        sq = G_['sq']
        ss4 = G_['ss4']
        rt4 = G_['rt4']
        rstd4 = G_['rstd4']
        on = G_['on']
        on2 = G_['on2']
        sgl = G_['sgl']
        yb = G_['yb']
        ybT = G_['ybT']
        for t in range(inst, NT, 2):
            b = (t // 2) % 2
            while ht_ready[0] <= t:
                yield
            HT = hT[t % NHT]
            pg1 = (yield from FP.get())
            proj_tm_(HT, pg1, O_GQ, 2 * F + V4)
            yield
            pg2 = (yield from FP.get())
            proj_tm_(HT, pg2, O_GA, 16 + V4)
            yield
            ht_done[t] += 1
            copy("vector", [pg2], [alow], alow[:], pg2[:, 0:16])
            yield
            I("scalar", "activation", [pg2], [sgl], out=sgl[:], in_=pg2[:, 16:16 + V4], func=AF.Exp, scale=-1.0)
            yield
            I("scalar", "activation", [sgl], [sgl], out=sgl[:], in_=sgl[:], func=AF.Ln, bias=1.0, scale=1.0)
            yield
            I("scalar", "activation", [sgl], [sgl], out=sgl[:], in_=sgl[:], func=AF.Exp, scale=-1.0)
            yield
            I("vector", "tensor_tensor", [pg2, sgl], [sgl], out=sgl[:], in0=pg2[:, 16:16 + V4], in1=sgl[:], op=ALU.mult)
            yield
            FP.put(pg2)
            AT = alT[b]
            pt = (yield from BP.get())
            transposes(P, pt, [(pt[0:16, 0:128], alow[:])], [alow, ident])
            yield
            copy("vector", [pt], [AT], AT[0:16, :], pt[0:16, 0:128])
            yield
            BP.put(pt)
            pz = (yield from FP.get())
            mm1(P, pz, pz[:, 0:F], AT[:], wa2[:], [AT, wa2])
            yield
            I("scalar", "activation", [pz], [e1], out=e1[:], in_=pz[:, 0:F], func=AF.Exp, scale=-1.0)
            yield
            I("scalar", "activation", [e1], [sp], out=sp[:], in_=e1[:], func=AF.Ln, bias=1.0, scale=1.0)
            yield
            mm1(P, pz, pz[:, 128:128 + F], cum[:], sp[:], [cum, sp])
            yield
            mm1(P, pz, pz[:, 256:256 + F], chones[:], sp[:], [chones, sp])
            yield
            mm1(P, pz, pz[0:F, 384:386], sp[:], chind[:], [chind, sp])
            yield
            I("scalar", "activation", [pz], [dec], out=dec[:], in_=pz[0:F, 384:386], func=AF.Exp, scale=-1.0 / 16.0)
            yield
            copy("scalar", [pz], [cbs], cbs[:], pz[:, 128:128 + F])
            yield
            I("vector", "scalar_tensor_tensor", [pz, cbs], [d1], out=d1[:], in0=pz[:, 256:256 + F], scalar=0.5, in1=cbs[:],
              op0=ALU.mult, op1=ALU.subtract)
            yield
            I("vector", "scalar_tensor_tensor", [pz, cbs], [d2], out=d2[:], in0=pz[:, 256:256 + F], scalar=-1.0, in1=cbs[:],
              op0=ALU.mult, op1=ALU.add)
            yield
            FP.put(pz)
            I("scalar", "activation", [d1], [E[0]], out=E[0][:], in_=d1[:], func=AF.Exp, scale=1.0 / 16.0)
            yield
            I("scalar", "activation", [d1], [E[1]], out=E[1][:], in_=d1[:], func=AF.Exp, scale=-1.0 / 16.0)
            yield
            I("scalar", "activation", [d2], [E[2]], out=E[2][:], in_=d2[:], func=AF.Exp, scale=1.0 / 16.0)
            yield
            I("scalar", "activation", [cbs], [E[3]], out=E[3][:], in_=cbs[:], func=AF.Exp, scale=-1.0 / 16.0)
            yield
            qsc = float(32 ** -0.5)
            I("vector", "scalar_tensor_tensor", [pg1, E[0]], [qt_], out=qt_[:], in0=pg1[:, 0:F], scalar=qsc, in1=E[0][:],
              op0=ALU.mult, op1=ALU.mult)
            yield
            I("vector", "scalar_tensor_tensor", [pg1, E[3]], [qh_], out=qh_[:], in0=pg1[:, 0:F], scalar=qsc, in1=E[3][:],
              op0=ALU.mult, op1=ALU.mult)
            yield
            I("vector", "tensor_tensor", [pg1, E[1]], [kt_], out=kt_[:], in0=pg1[:, F:2 * F], in1=E[1][:], op=ALU.mult)
            yield
            I("vector", "tensor_tensor", [pg1, E[2]], [kh_], out=kh_[:], in0=pg1[:, F:2 * F], in1=E[2][:], op=ALU.mult)
            yield
            copy("scalar", [pg1], [vbf], vbf[:], pg1[:, 2 * F:2 * F + V4])
            yield
            FP.put(pg1)
            pt = (yield from BP.get())
            transposes(P, pt, [(pt[0:32, hh * 128:(hh + 1) * 128], qt_[:, hh * 32:(hh + 1) * 32]) for hh in range(NH)] +
                       [(pt[0:32, 512 + hh * 128:512 + (hh + 1) * 128], kt_[:, hh * 32:(hh + 1) * 32]) for hh in range(NH)],
                       [qt_, kt_, ident])
            yield
            copy("vector", [pt], [qtT], qtT[:], vw(pt.t, 0, 32, 0, (128, NH), (1, 128)))
            yield
            copy("scalar", [pt], [ktT], ktT[:], vw(pt.t, 0, 32, 512, (128, NH), (1, 128)))
            yield
            BP.put(pt)
            pt = (yield from BP.get())
            transposes(P, pt, [(pt[0:F, 0:128], qh_[:])], [qh_, ident])
            yield
            copy("vector", [pt], [qhT], qhT[:], pt[0:F, 0:128])
            yield
            BP.put(pt)
            patt = (yield from FP.get())
            for hh in range(NH):
                mm1(P, patt, patt[:, hh * 128:(hh + 1) * 128], ktT[:, hh, :], qtT[:, hh, :], [ktT, qtT])
                yield
            I("vector", "tensor_tensor", [patt, gmask], [attm], out=attm[:], in0=vw(patt.t, 0, 128, 0, (128, NH), (1, 128)),
              in1=vw(gmask.t, 0, 128, 0, (0, NH), (1, 128)), op=ALU.mult)
            yield
            FP.put(patt)
            while st['ready'] < t:
                yield
            Sa_f, Sa_b = st['Sa_f'], st['Sa_b']
            pds = (yield from FP.get())
            pds1 = (yield from FP.get())
            mm1(P, pds, pds[0:F, 0:V4], kh_[0:64, :], vbf[0:64, :], [kh_, vbf])
            yield
            mm1(P, pds1, pds1[0:F, 0:V4], kh_[64:128, :], vbf[64:128, :], [kh_, vbf])
            yield
            Sm_f, Sm_b = Sf.next(), Sb.next()
            I("vector", "tensor_tensor", [pds, bdm], [tmpS], out=tmpS[0:F, :], in0=pds[0:F, 0:V4], in1=bdm[0:F, 0:V4], op=ALU.mult)
            yield
            FP.put(pds)
            I("vector", "scalar_tensor_tensor", [Sa_f, dec, tmpS], [Sm_f], out=Sm_f[0:F, :], in0=Sa_f[0:F, :], scalar=dec[:, 0:1],
              in1=tmpS[0:F, :], op0=ALU.mult, op1=ALU.add)
            yield
            copy("gpsimd", [Sm_f], [Sm_b], Sm_b[0:F, :], Sm_f[0:F, :])
            yield
            po = (yield from FP.get())
            for hh in range(NH):
                mm1(P, po, po[:, hh * 64:(hh + 1) * 64], attm[:, hh, :], vbf[:, hh * 64:(hh + 1) * 64], [attm, vbf],
                    start=(hh == 0), stop=False, skip=True)
                yield
            mm1(P, po, po[0:64, 0:V4], qhT[:, 0:64], Sa_b[0:F, :], [qhT, Sa_b], start=False, stop=False, skip=True)
            yield
            mm1(P, po, po[64:128, 0:V4], qhT[:, 64:128], Sm_b[0:F, :], [qhT, Sm_b], start=False, stop=True, skip=True)
            yield
            Sn_f, Sn_b = Sf.next(), Sb.next()
            I("vector", "tensor_tensor", [pds1, bdm], [tmpS], out=tmpS[0:F, :], in0=pds1[0:F, 0:V4], in1=bdm[0:F, 0:V4], op=ALU.mult)
            yield
            FP.put(pds1)
            I("vector", "scalar_tensor_tensor", [Sm_f, dec, tmpS], [Sn_f], out=Sn_f[0:F, :], in0=Sm_f[0:F, :], scalar=dec[:, 1:2],
              in1=tmpS[0:F, :], op0=ALU.mult, op1=ALU.add)
            yield
            copy("gpsimd", [Sn_f], [Sn_b], Sn_b[0:F, :], Sn_f[0:F, :])
            yield
            st['Sa_f'], st['Sa_b'] = Sn_f, Sn_b
            st['ready'] = t + 1
            I("scalar", "activation", [po], [sq], out=sq[:], in_=po[:, 0:V4], func=AF.Square, scale=0.125)
            yield
            I("vector", "tensor_reduce", [sq], [ss4], out=ss4[:], in_=vw(sq.t, 0, 128, 0, (64, NH), (1, 64)), axis=AX.X, op=ALU.add)
            yield
            I("scalar", "activation", [ss4], [rt4], out=rt4[:], in_=ss4[:], func=AF.Ln, bias=EPS, scale=1.0)
            yield
            I("scalar", "activation", [rt4], [rstd4], out=rstd4[:], in_=rt4[:], func=AF.Exp, scale=-0.5)
            yield
            I("vector", "tensor_tensor", [po, rstd4], [on], out=vw(on.t, 0, 128, 0, (64, NH), (1, 64)),
              in0=vw(po.t, 0, 128, 0, (64, NH), (1, 64)), in1=vw(rstd4.t, 0, 128, 0, (1, NH), (0, 64)), op=ALU.mult)
            yield
            FP.put(po)
            I("gpsimd", "tensor_tensor", [on, ghn], [on2], out=vw(on2.t, 0, 128, 0, (64, NH), (1, 64)),
              in0=vw(on.t, 0, 128, 0, (64, NH), (1, 64)), in1=vw(ghn.t, 0, 128, 0, (0, NH), (1, 64)), op=ALU.mult)
            yield
            I("vector", "tensor_tensor", [on2, sgl], [yb], out=yb[:], in0=on2[:], in1=sgl[:], op=ALU.mult)
            yield
            YT = ybT[b]
            pt = (yield from BP.get())
            transposes(P, pt, [(pt[:, c * 128:(c + 1) * 128], yb[:, c * 128:(c + 1) * 128]) for c in range(NJ)], [yb, ident])
            yield
            copy("scalar", [pt], [YT], YT[:], vw(pt.t, 0, 128, 0, (128, NJ), (1, 128)))
            yield
            BP.put(pt)
            P.dma("sync", bass.AP(dr["ysend"], V4 * S + t * 128, [[S, 128], [128 * S, NJ], [1, 128]]), YT[:], reads=[YT])
            yield


    def moba_chain(inst):
        O_ = OB[inst]
        qktm = O_['qktm']
        rtmp = O_['rtmp']
        vtc = O_['vtc']
        qkTs = O_['qkTs']
        for t in range(inst, NT, NMOBA):
            b = (t // NMOBA) % 2
            while ht_ready[0] <= t:
                yield
            HT = hT[t % NHT]
            pb1 = (yield from FP.get())
            proj_tm_(HT, pb1, O_BQ, 2 * V4)
            yield
            pb2 = (yield from FP.get())
            proj_tm_(HT, pb2, O_BV, V4)
            yield
            ht_done[t] += 1
            x1 = vw(pb1.t, 0, 128, 0, (64, 2 * NH), (1, 8))
            x2 = vw(pb1.t, 0, 128, 8, (64, 2 * NH), (1, 8))
            cs8 = vw(cosC.t, 0, 128, t * 8, (0, 2 * NH), (1, 8))
            sn8 = vw(sinC.t, 0, 128, t * 8, (0, 2 * NH), (1, 8))
            I("vector", "tensor_tensor", [pb1, cosC], [rtmp[0]], out=rtmp[0][:], in0=x1, in1=cs8, op=ALU.mult)
            yield
            I("vector", "tensor_tensor", [pb1, sinC], [rtmp[1]], out=rtmp[1][:], in0=x2, in1=sn8, op=ALU.mult)
            yield
            I("vector", "tensor_tensor", [rtmp[0], rtmp[1]], [qktm], out=qktm[:, :, 0:8], in0=rtmp[0][:], in1=rtmp[1][:], op=ALU.subtract)
            yield
            I("vector", "tensor_tensor", [pb1, cosC], [rtmp[2]], out=rtmp[2][:], in0=x2, in1=cs8, op=ALU.mult)
            yield
            I("vector", "tensor_tensor", [pb1, sinC], [rtmp[3]], out=rtmp[3][:], in0=x1, in1=sn8, op=ALU.mult)
            yield
            I("vector", "tensor_tensor", [rtmp[2], rtmp[3]], [qktm], out=qktm[:, :, 8:16], in0=rtmp[2][:], in1=rtmp[3][:], op=ALU.add)
            yield
            copy("scalar", [pb1], [qktm], qktm[:, :, 16:64], vw(pb1.t, 0, 128, 16, (64, 2 * NH), (1, 48)))
            yield
            FP.put(pb1)
            VC = vtc[b]
            copy("scalar", [pb2], [VC], VC[:, :, 0:64], vw(pb2.t, 0, 128, 0, (64, NH), (1, 64)))
            yield
            FP.put(pb2)
            P.dma("sync", bass.AP(dr["v_c"], t * 128 * NH * 65, [[NH * 65, 128], [1, NH * 65]]), VC[:], reads=[VC])
            yield
            QK = qkTs[b]
            for half in range(2):
                pt = (yield from BP.get())
                transposes(P, pt, [(pt[0:64, hh * 128:(hh + 1) * 128], qktm[:, half * NH + hh, :]) for hh in range(NH)],
                           [qktm, ident])
                yield
                copy(evac.next(), [pt], [QK], QK[:, half * NH:(half + 1) * NH, :], vw(pt.t, 0, 64, 0, (128, NH), (1, 128)))
                yield
                BP.put(pt)
            P.dma("sync", bass.AP(dr["qT_c"], t * 128, [[S, 64], [64 * S, NH], [1, 128]]), QK[:, 0:NH, :], reads=[QK])
            yield
            P.dma("sync", bass.AP(dr["kT_c"], t * 128, [[S, 64], [64 * S, NH], [1, 128]]), QK[:, NH:2 * NH, :], reads=[QK])
            yield


    def lru_chain():
        for t in range(NT):
            b = t % 2
            while ht_ready[0] <= t:
                yield
            HT = hT[t % NHT]
            pr = (yield from FP.get())
            for j in range(NJ):
                mm_group(P, pr, [(pr[:, j * 128:(j + 1) * 128], win[:, c, O_RX + j * 128:O_RX + (j + 1) * 128], HT[:, c, :])
                                 for c in range(8)], [HT, win])
                yield
            for j in range(NJ):
                mm_group(P, pr, [(pr[:, 256 + j * 128:256 + (j + 1) * 128], win[:, c, O_RG + j * 128:O_RG + (j + 1) * 128], HT[:, c, :])
                                 for c in range(8)], [HT, win])
                yield
            XP, XPprev = xpad[b], xpad[1 - b]
            copy("scalar", [pr], [XP], XP[:, :, 3:131], vw(pr.t, 0, 128, 0, (128, NJ), (1, 128)))
            yield
            ht_done[t] += 1
            I("scalar", "activation", [pr], [gl], out=gl[:], in_=vw(pr.t, 0, 128, 256, (128, NJ), (1, 128)), func=AF.Gelu)
            yield
            FP.put(pr)
            copy("gpsimd", [XPprev], [XP], XP[:, :, 0:3], XPprev[:, :, 128:131])
            yield
            for j in range(NJ):
                I("vector", "tensor_scalar", [XP, cw, cbv], [xc], out=xc[:, j, :], in0=XP[:, j, 0:128], scalar1=cw[:, 0, j:j + 1],
                  scalar2=cbv[:, j:j + 1], op0=ALU.mult, op1=ALU.add)
                yield
                for k in range(1, 4):
                    I("vector", "scalar_tensor_tensor", [XP, cw, xc], [xc], out=xc[:, j, :], in0=XP[:, j, k:k + 128],
                      scalar=cw[:, k, j:j + 1], in1=xc[:, j, :], op0=ALU.mult, op1=ALU.add)
                    yield
            copy("gpsimd", [xc], [xcb], xcb[:], xc[:])
            yield
            pgt = (yield from FP.get())
            for j in range(NJ):
                mm1(P, pgt, pgt[:, j * 128:(j + 1) * 128], BDa[:, j, :], xcb[:, j, :], [BDa, xcb])
                yield
                mm1(P, pgt, pgt[:, 256 + j * 128:256 + (j + 1) * 128], BDx[:, j, :], xcb[:, j, :], [BDx, xcb])
                yield
            for j in range(NJ):
                I("scalar", "activation", [pgt, nbab], [r_], out=r_[:, j, :], in_=pgt[:, j * 128:(j + 1) * 128], func=AF.Exp,
                  bias=nbab[:, j:j + 1], scale=-1.0)
                yield
                I("scalar", "activation", [pgt, nbxb], [i_], out=i_[:, j, :], in_=pgt[:, 256 + j * 128:256 + (j + 1) * 128],
                  func=AF.Exp, bias=nbxb[:, j:j + 1], scale=-1.0)
                yield
            FP.put(pgt)
            for g_ in (r_, i_):
                I("scalar", "activation", [g_], [g_], out=g_[:], in_=g_[:], func=AF.Ln, bias=1.0, scale=1.0)
                yield
                I("scalar", "activation", [g_], [g_], out=g_[:], in_=g_[:], func=AF.Exp, scale=-1.0)
                yield
            for j in range(NJ):
                I("scalar", "activation", [r_, cc], [a_], out=a_[:, j, :], in_=r_[:, j, :], func=AF.Exp, scale=cc[:, j:j + 1])
                yield
            I("scalar", "activation", [a_], [a2], out=a2[:], in_=a_[:], func=AF.Square)
            yield
            I("scalar", "activation", [a2], [ug], out=ug[:], in_=a2[:], func=AF.Ln, scale=-1.0, bias=1.0)
            yield
            I("scalar", "activation", [ug], [ug], out=ug[:], in_=ug[:], func=AF.Exp, scale=0.5)
            yield
            I("vector", "tensor_tensor", [i_, xc], [m_], out=m_[:], in0=i_[:], in1=xc[:], op=ALU.mult)
            yield
            I("gpsimd", "tensor_tensor", [ug, m_], [u_], out=u_[:], in0=ug[:], in1=m_[:], op=ALU.mult)
            yield
            HS, HSprev = hs[b], hs[1 - b]
            for j in range(NJ):
                init = 0.0 if t == 0 else HSprev[:, j, 127:128]
                I("vector", "tensor_tensor_scan", [a_, u_] + ([] if t == 0 else [HSprev]), [HS], out=HS[:, j, :],
                  data0=a_[:, j, :], data1=u_[:, j, :], initial=init, op0=ALU.mult, op1=ALU.add)
                yield
            YD = yd[b]
            I("vector", "tensor_tensor", [HS, gl], [YD], out=YD[:], in0=HS[:], in1=gl[:], op=ALU.mult)
            yield
            P.dma("sync", bass.AP(dr["ysend"], 3 * V4 * S + t * 128, [[S, 128], [128 * S, NJ], [1, 128]]), YD[:], reads=[YD])
            yield


    gens = [norm_chain(), *[mla_chain(k) for k in range(NMLA)], gla_chain(0), gla_chain(1),
            *[moba_chain(k) for k in range(NMOBA)], lru_chain()]
    alive = list(gens)
    idle_rounds = 0
    while alive:
        n0 = P.ninst
        for g in list(alive):
            try:
                next(g)
            except StopIteration:
                alive.remove(g)
        idle_rounds = idle_rounds + 1 if P.ninst == n0 else 0
        assert idle_rounds < 1000, "phase A scheduler deadlock"
    drain_dma(P)
    P.flush()
    C.close()


def phase_b(nc, P, S, dr, NH=4):
    assert NH == 2
    C = Ctx(nc, P)
    I = P.I
    NT, NQ, NB = S // 128, S // 512, S // 256
    ident = C.sb("ident", [128, 128], BF16)
    P.ident = ident
    tri = C.sb("tri", [128, 896], BF16)
    ones_f = C.sb("ones_f", [128, 64], F32)
    PTs = [Ring([C.sb(f"PT{j}_{i}", [128, 512], BF16) for i in range(3)]) for j in range(2)]
    rrows = [Ring([C.sb(f"rrow{j}_{i}", [128, 512], F32) for i in range(2)]) for j in range(2)]
    oTs = [Ring([C.sb(f"oT{j}_{i}", [65, 512], F32) for i in range(2)]) for j in range(2)]
    yts = [Ring([C.sb(f"yt{j}_{i}", [64, 512], BF16) for i in range(2)]) for j in range(2)]
    pSs = [[C.ps(f"pS{j}_{i}") for i in range(2)] for j in range(2)]
    pOs = [C.ps(f"pO{j}") for j in range(2)]
    pbc = C.ps("pbc")
    pgate = pbc
    ptr = C.ps("ptr", BF16)
    P.dma("sync", ident[:], dr["c_ident"].ap(), writes=[ident])
    P.dma("sync", tri[:], dr["c_tri"].ap(), writes=[tri])
    I("gpsimd", "memset", [], [ones_f], ones_f[:], 1.0)

    def mixer(kind):
        moba = (kind == "c")
        dq = 64 if moba else 96
        KR = dq + (16 if moba else 0)
        scale = float(dq ** -0.5)
        yoff = 2 * NH * 64 if moba else 0
        qT_d, kT_d, v_d = dr["qT_" + kind], dr["kT_" + kind], dr["v_" + kind]
        QT = [C.sb(f"QT{kind}{i}", [KR, S], BF16) for i in range(2)]
        KT = [C.sb(f"KT{kind}{i}", [KR, S], BF16) for i in range(2)]
        V = C.sb(f"V{kind}", [128, NT, NH, 128], BF16)
        I("gpsimd", "memset", [], [V], V[:], 0.0)
        for j in range(2):
            P.dma("sync", QT[j][0:dq, :], bass.AP(qT_d, j * dq * S, [[S, dq], [1, S]]), writes=[QT[j]])
            P.dma("sync", KT[j][0:dq, :], bass.AP(kT_d, j * dq * S, [[S, dq], [1, S]]), writes=[KT[j]])
        for hh_ in range(NH):
            P.dma("sync", V[:, :, hh_, 0:65], bass.AP(v_d, hh_ * 65, [[NH * 65, 128], [128 * NH * 65, NT], [1, 65]]), writes=[V])
        if moba:
            ksum = [C.sb(f"ksum{j}", [64, 16], F32) for j in range(2)]
            kmT = [C.sb(f"kmT{j}", [64, 16], BF16) for j in range(2)]
            gms = [C.sb(f"gm{j}", [128, 16], F32) for j in range(2)]
            mx8s = [C.sb(f"mx8{j}", [128, 8], F32) for j in range(2)]
            sels = [C.sb(f"sel{j}", [128, 16], F32) for j in range(2)]
            selbs = [Ring([C.sb(f"selb{j}_{i}", [128, 80], BF16) for i in range(2)]) for j in range(2)]
            for r_ in selbs:
                for sb_ in r_.items:
                    I("gpsimd", "memset", [], [sb_], sb_[:], 0.0)
            for kt_ in KT:
                P.dma("sync", kt_[64:80, :], dr["c_onehot"].ap(), writes=[kt_])

        def gate_chain(j):
            Q, K = QT[j], KT[j]
            gm, mx8, sel = gms[j], mx8s[j], sels[j]
            I("vector", "tensor_reduce", [K], [ksum[j]], out=ksum[j][:, 0:NB], in_=vw(K.t, 0, 64, 0, (256, NB), (1, 256)),
              axis=AX.X, op=ALU.add)
            yield
            I("vector", "tensor_scalar_mul", [ksum[j]], [kmT[j]], kmT[j][:, 0:NB], ksum[j][:, 0:NB], 1.0 / 256.0)
            yield
            for qi in range(NT):
                blk = qi // 2
                qs = slice(qi * 128, (qi + 1) * 128)
                sb_ = selbs[j].next()
                if blk <= 3:
                    I("gpsimd", "memset", [], [sb_], sb_[:, 64:80], NEGBIG)
                    I("gpsimd", "memset", [], [sb_], sb_[:, 64:64 + blk + 1], 0.0)
                    yield
                else:
                    mm1(P, pgate, pgate[:, 0:NB], Q[0:64, qs], kmT[j][:, 0:NB], [Q, kmT[j]])
                    I("gpsimd", "memset", [], [gm], gm[:], -1.0e30)
                    I("vector", "tensor_copy", [pgate], [gm], gm[:, 0:blk], pgate[:, 0:blk])
                    yield
                    I("vector", "max", [gm], [mx8], out=mx8[:], in_=gm[:])
                    yield
                    I("vector", "tensor_scalar", [gm, mx8], [sel], out=sel[:], in0=gm[:], scalar1=mx8[:, 2:3], scalar2=None,
                      op0=ALU.is_ge)
                    yield
                    I("vector", "tensor_scalar", [sel], [sb_], out=sb_[:, 64:80], in0=sel[:], scalar1=-1.0, scalar2=-NEGBIG,
                      op0=ALU.add, op1=ALU.mult)
                    yield
                    I("vector", "memset", [], [sb_], sb_[:, 64 + blk:64 + blk + 1], 0.0)
                    yield
                    yield
                    yield
                transposes(P, ptr, [(ptr[0:80, 0:128], sb_[:])], [sb_, ident])
                I("vector", "tensor_copy", [ptr], [Q], Q[64:80, qs], ptr[64:80, 0:128])
                yield

        def finalize_chain(j, hh, t, o65):
            rr = rrows[j].next()
            I("vector", "reciprocal", [o65], [rr], rr[64:65, :], o65[64:65, :])
            for _ in range(10):
                yield
            mm1(P, pbc, pbc[0:64, :], ones_f[64:65, 0:64], rr[64:65, :], [ones_f, rr])
            y_ = yts[j].next()
            I("vector", "tensor_tensor", [o65, pbc], [y_], out=y_[:], in0=o65[0:64, :], in1=pbc[0:64, :], op=ALU.mult)
            yield
            P.dma("gpsimd", bass.AP(dr["ysend"], (yoff + hh * 64) * S + t * 512, [[S, 64], [1, 512]]), y_[:], reads=[y_])
            yield

        def attn_chain(j, hh):
            Q, K = QT[j], KT[j]
            po = pOs[j]
            sb2 = pSs[j]
            for t in range(NQ):
                nk = 4 * t + 4
                qs = slice(t * 512, (t + 1) * 512)

                def s_mm(kt):
                    c0 = 128 * max(0, kt - 4 * t)
                    ps_ = sb2[kt % 2]
                    mm1(P, ps_, ps_[:, c0:512], K[:, kt * 128:(kt + 1) * 128], Q[:, t * 512 + c0:(t + 1) * 512], [K, Q])

                s_mm(0)
                yield
                for kt in range(nk):
                    if kt + 1 < nk:
                        s_mm(kt + 1)
                        yield
                    ps_ = sb2[kt % 2]
                    pt_ = PTs[j].next()
                    c0 = 128 * max(0, kt - 4 * t)
                    I("scalar", "activation", [ps_], [pt_], out=pt_[:, c0:512], in_=ps_[:, c0:512], func=AF.Exp, scale=scale)
                    yield
                    if kt >= 4 * t:
                        I("gpsimd", "tensor_tensor", [pt_, tri], [pt_], out=pt_[:, c0:c0 + 128], in0=pt_[:, c0:c0 + 128],
                          in1=tri[:, 384:512], op=ALU.mult)
                        yield
                    P.I("tensor", "matmul", [V, pt_], [po], po[:, c0:512], lhsT=V[:, kt, hh, :],
                        rhs=pt_[:, c0:512], start=(kt == 0), stop=(kt == nk - 1))
                    yield
                o65 = oTs[j].next()
                I("scalar", "copy", [po], [o65], out=o65[:], in_=po[0:65, :])
                yield finalize_chain(j, hh, t, o65)

        return gate_chain if moba else None, attn_chain

    _, attn_a = mixer("a")
    gate_c, attn_c = mixer("c")
    run_chains(P, [attn_a(0, 0), attn_a(1, 1), gate_c(0), gate_c(1)], "MLA attention + MoBA gating")
    run_chains(P, [attn_c(0, 0), attn_c(1, 1)], "MoBA attention")
    drain_dma(P)
    P.flush()
    C.close()


SEQ = 4096
DEPTH = 2
NHC = 2
WNAMES = ["attn_norm", "w_in", "mla_q_norm", "mla_w_uq", "mla_kv_norm", "mla_w_ukv", "gla_w_a2", "gla_b_a2",
          "gla_head_norm", "lru_conv_w", "lru_conv_b", "lru_w_a", "lru_b_a", "lru_w_x", "lru_b_x", "lru_lambda",
          "w_out", "ffn_norm", "w_ffn_in", "w_ffn_out", "final_norm"]
PAIRS = [[0, 1], [2, 3], [4, 5], [6, 7]]


def make_consts(S):
    bf = ml_dtypes.bfloat16
    c = {}
    c["c_ident"] = np.eye(128, dtype=np.float32).astype(bf)
    pos = np.arange(S, dtype=np.float32)[:, None]
    for nm, dim in (("A", 32), ("C", 16)):
        inv = (np.float32(500000.0) ** (-np.arange(0, dim, 2, dtype=np.float32) / np.float32(dim))).astype(np.float32)
        ang = (pos * inv[None, :]).astype(np.float32)
        c["c_cos" + nm] = np.cos(ang).astype(np.float32)
        c["c_sin" + nm] = np.sin(ang).astype(np.float32)
    k = np.arange(128)[:, None]
    cc = np.arange(896)[None, :]
    c["c_tri"] = ((cc - 384) >= k).astype(np.float32).astype(bf)
    j = np.arange(128)[:, None]
    i = np.arange(128)[None, :]
    same = (j // 64) == (i // 64)
    c["c_cum"] = (same & (j <= i)).astype(np.float32)
    c["c_gmask"] = c["c_cum"].astype(bf)
    c["c_chones"] = same.astype(np.float32)
    c["c_chind"] = ((np.arange(128)[:, None] // 64) == np.arange(2)[None, :]).astype(np.float32)
    c["c_bdm"] = ((np.arange(128)[:, None] // 32) == (np.arange(256)[None, :] // 64)).astype(np.float32)
    c["c_onehot"] = ((np.arange(S)[None, :] // 256) == np.arange(16)[:, None]).astype(np.float32).astype(bf)
    return c


def pack_core(inp, r, NH):
    f = lambda a: np.ascontiguousarray(np.asarray(a, dtype=np.float32))
    if NH == 4:
        return {nm: f(inp[nm]) for nm in WNAMES}
    h0 = r * NH
    F, V4, NJ = NH * 32, NH * 64, NH // 2
    w_in = np.asarray(inp["w_in"])
    cols = np.concatenate([
        np.arange(0, 416),
        O_GQ + h0 * 32 + np.arange(F), O_GK + h0 * 32 + np.arange(F), O_GV + h0 * 64 + np.arange(V4),
        O_GA + np.arange(16), O_GG + h0 * 64 + np.arange(V4),
        O_BQ + h0 * 64 + np.arange(V4), O_BK + h0 * 64 + np.arange(V4), O_BV + h0 * 64 + np.arange(V4),
        O_RX + r * NJ * 128 + np.arange(NJ * 128), O_RG + r * NJ * 128 + np.arange(NJ * 128)])
    o = {}
    for nm in ("attn_norm", "mla_q_norm", "mla_kv_norm", "gla_head_norm", "ffn_norm", "w_ffn_in", "w_ffn_out", "final_norm"):
        o[nm] = f(inp[nm])
    o["w_in"] = f(w_in[:, :, cols])
    o["mla_w_uq"] = f(np.asarray(inp["mla_w_uq"])[:, :, h0 * 96:(h0 + NH) * 96])
    o["mla_w_ukv"] = f(np.asarray(inp["mla_w_ukv"])[:, :, h0 * 128:(h0 + NH) * 128])
    o["gla_w_a2"] = f(np.asarray(inp["gla_w_a2"])[:, :, h0 * 32:(h0 + NH) * 32])
    o["gla_b_a2"] = f(np.asarray(inp["gla_b_a2"])[:, h0 * 32:(h0 + NH) * 32])
    ch = slice(r * NJ * 128, (r + 1) * NJ * 128)
    o["lru_conv_w"] = f(np.asarray(inp["lru_conv_w"])[:, :, :, ch])
    for nm in ("lru_conv_b", "lru_b_a", "lru_b_x", "lru_lambda"):
        o[nm] = f(np.asarray(inp[nm])[:, ch])
    for nm in ("lru_w_a", "lru_w_x"):
        o[nm] = f(np.asarray(inp[nm])[:, r * 2 * NJ:(r + 1) * 2 * NJ])
    nr = 4 // NH
    perm = []
    for c in range(4 * V4 // 256):
        for rr in range(nr):
            row = c * 256 + np.arange(256)
            g, idx = row // V4, row % V4
            perm.append(g * 256 + rr * V4 + idx)
    perm = np.concatenate(perm)
    o["w_out"] = f(np.asarray(inp["w_out"])[:, perm, :])
    return o


def all_gather(P, src, dst, rows, chunk):
    for c in range(rows // chunk):
        P.cccnt += 1
        cnt = P.cccnt

        def thunk(e, cnt=cnt, c=c):
            e.collective_compute("AllGather", ALU.bypass, replica_groups=PAIRS, ins=[src[c * chunk:(c + 1) * chunk, :]],
                                 outs=[dst[2 * c * chunk:2 * (c + 1) * chunk, :]]).then_inc(P.ccsem, 1)
            e.wait_ge(P.ccsem, cnt)
        P.raw("gpsimd", thunk)
    P.flush()


def build_program(S, shapes, consts, NH):
    nc = bass.Bass("TRN2", target_bir_lowering=False)
    pair = (NH == 2)
    SH = S // 2 if pair else S
    V4 = NH * 64
    dr = {}
    dr["x"] = nc.dram_tensor("x", [S, D], F32, kind="ExternalInput")
    for nm in WNAMES:
        dr[nm] = nc.dram_tensor(nm, list(shapes[nm]), F32, kind="ExternalInput")
    for nm, v in consts.items():
        dr[nm] = nc.dram_tensor(nm, list(v.shape), BF16 if v.dtype == ml_dtypes.bfloat16 else F32, kind="ExternalInput")
    dr["out"] = nc.dram_tensor("out", [SH, D], F32, kind="ExternalOutput")
    dr["qT_a"] = nc.dram_tensor("qT_a", [NH, 96, S], BF16)
    dr["kT_a"] = nc.dram_tensor("kT_a", [NH, 96, S], BF16)
    dr["v_a"] = nc.dram_tensor("v_a", [S, NH * 65], BF16)
    dr["qT_c"] = nc.dram_tensor("qT_c", [NH, 64, S], BF16)
    dr["kT_c"] = nc.dram_tensor("kT_c", [NH, 64, S], BF16)
    dr["v_c"] = nc.dram_tensor("v_c", [S, NH * 65], BF16)
    dr["yT"] = nc.dram_tensor("yT", [D, S], BF16)
    P = Prog(nc)
    if pair:
        dr["xh"] = nc.dram_tensor("xh", [SH, D], F32, kind="ExternalInput")
        dr["roff"] = nc.dram_tensor("roff", [1, 1], mybir.dt.int32, kind="ExternalInput")
        dr["ysend"] = nc.dram_tensor("ysend", [4 * V4, S], BF16)
        dr["xhalf"] = nc.dram_tensor("xhalf", [SH, D], F32)
        dr["xfull"] = nc.dram_tensor("xfull", [S, D], F32)
        P.sync_reg = nc.sync.alloc_register("roff_reg")
    else:
        dr["ysend"] = dr["yT"]
        dr["xres"] = nc.dram_tensor("xres", [S, D], F32)
    for l in range(DEPTH):
        last = (l == DEPTH - 1)
        if pair:
            dr["xin"] = dr["x"] if l == 0 else dr["xfull"]
            dr["xin_rowmap"] = None if l == 0 else (lambda t0: ((t0 % SH) // 512) * 1024 + (t0 // SH) * 512 + (t0 % 512))
            dr["xc_in"] = dr["xh"] if l == 0 else dr["xhalf"]
            dr["xout"] = dr["out"] if last else dr["xhalf"]
        else:
            dr["xin"] = dr["x"] if l == 0 else dr["xres"]
            dr["xc_in"] = dr["xin"]
            dr["xout"] = dr["out"] if last else dr["xres"]
        phase_a(nc, P, S, l, dr, NH)
        phase_b(nc, P, S, dr, NH)
        if pair:
            all_gather(P, dr["ysend"], dr["yT"], 4 * V4, 256)
        phase_c(nc, P, S, l, dr, SH, last, dyn=pair)
        if pair and not last:
            all_gather(P, dr["xhalf"], dr["xfull"], SH, 512)
    return nc, P


_CACHE = {}


def kernel(**inputs):
    x = np.asarray(inputs["x"], dtype=np.float32)
    B, S, _ = x.shape
    NH = NHC
    nr = 4 // NH
    consts = make_consts(S)
    packed = [pack_core(inputs, r, NH) for r in range(nr)]
    shapes = {nm: packed[0][nm].shape for nm in WNAMES}
    key = (S, NH, tuple(sorted(shapes.items())))
    if key not in _CACHE:
        _CACHE[key] = build_program(S, shapes, consts, NH)
    nc, _ = _CACHE[key]
    SH = S // nr
    in_maps = []
    for b in range(B):
        for r in range(nr):
            m = {"x": np.ascontiguousarray(x[b])}
            m.update(packed[r])
            m.update(consts)
            if nr == 2:
                m["xh"] = np.ascontiguousarray(x[b, r * SH:(r + 1) * SH])
                m["roff"] = np.array([[r * SH]], dtype=np.int32)
            in_maps.append(m)
    res = run_bass_kernel_spmd(nc, in_maps, core_ids=list(range(B * nr)))
    out = np.empty((B, S, D), dtype=np.float32)
    for b in range(B):
        for r in range(nr):
            out[b, r * SH:(r + 1) * SH] = np.asarray(res.results[b * nr + r]["out"], dtype=np.float32)
    return out
```

```python
import numpy as np
import ml_dtypes
import concourse.bass as bass
import concourse.mybir as mybir
from concourse.bass_utils import run_bass_kernel_spmd

F32 = mybir.dt.float32
BF16 = mybir.dt.bfloat16
AF = mybir.ActivationFunctionType
ALU = mybir.AluOpType
AX = mybir.AxisListType

ENGS = ("tensor", "vector", "scalar", "gpsimd", "sync")
import os as _os
STRICT_SAME_ENGINE = bool(int(_os.environ.get("STRICT", "1")))

DBG_SKIP_GLA = bool(int(_os.environ.get('DBG_SKIP_GLA', '0')))
DBG_GLA_STAGE = int(_os.environ.get('DBG_GLA_STAGE', '99'))


class Buf:
    def __init__(self, name, t=None, excl=False):
        self.name = name
        self.t = t
        self.excl = excl
        self.writer = []
        self.readers = []

    def __getitem__(self, idx):
        return self.t[idx]


class Prog:
    def __init__(self, nc, dma_ring=24):
        self.nc = nc
        self.q = {e: [] for e in ENGS}
        self.sem = {e: nc.alloc_semaphore(name=f"s_{e}") for e in ENGS}
        self.cnt = {e: 0 for e in ENGS}
        self.waited = {e: {} for e in ENGS}
        self.ring = {}
        for e in ("sync", "gpsimd", "scalar"):
            n = dma_ring if e != "scalar" else 8
            self.ring[e] = {"sems": [nc.alloc_semaphore(name=f"d_{e}{i}") for i in range(n)],
                            "val": [0] * n, "pos": 0}
        self.ninst = 0
        self.ccsem = nc.alloc_semaphore(name="s_cc")
        self.cccnt = 0
        self.roff_val = None

    def raw(self, eng, fn):
        self.q[eng].append(fn)

    def _deps(self, reads, writes, extra):
        deps = list(extra)
        for b in reads:
            deps += b.writer
        for b in writes:
            deps += b.writer
            deps += b.readers
        return deps

    def _commit(self, tok, reads, writes):
        for b in reads:
            b.readers.append(tok)
            if len(b.readers) > 64:
                b.readers = b.readers[-64:]
        for b in writes:
            b.writer = [tok]
            b.readers = []

    def _waits(self, eng, deps):
        out = []
        w = self.waited[eng]
        best = {}
        for (s, v, e_src) in deps:
            if e_src == eng and (eng == "tensor" or not STRICT_SAME_ENGINE):
                continue
            k = id(s)
            if w.get(k, 0) >= v:
                continue
            if k not in best or best[k][1] < v:
                best[k] = (s, v)
        for k, (s, v) in best.items():
            w[k] = v
            out.append((s, v))
        return out

    def op(self, eng, fn, reads=(), writes=(), deps=(), signal=True):
        if any(b.excl for b in reads):
            writes = list(writes) + [b for b in reads if b.excl and b not in writes]
            reads = [b for b in reads if not b.excl]
        d = self._deps(reads, writes, deps)
        waits = self._waits(eng, d)
        if signal:
            self.cnt[eng] += 1
            tok = (self.sem[eng], self.cnt[eng], eng)
        else:
            tok = None
        sem = self.sem[eng]

        def thunk(e, fn=fn, waits=waits, signal=signal, sem=sem):
            for (s, v) in waits:
                e.wait_ge(s, v)
            ins = fn(e)
            if signal:
                ins.then_inc(sem, 1)
        self.q[eng].append(thunk)
        self.ninst += 1
        if tok is not None:
            self._commit(tok, reads, writes)
        return tok

    def I(self, eng, name, reads, writes, *args, signal=True, **kw):
        return self.op(eng, lambda e, name=name, args=args, kw=kw: getattr(e, name)(*args, **kw),
                       reads=reads, writes=writes, signal=signal)

    def dma(self, eng, out, in_, reads=(), writes=(), deps=(), **kw):
        r = self.ring[eng]
        i = r["pos"]
        r["pos"] = (i + 1) % len(r["sems"])
        s = r["sems"][i]
        d = self._deps(reads, writes, deps)
        if r["val"][i] > 0:
            d.append((s, r["val"][i], "dma"))
        waits = self._waits(eng, d)
        r["val"][i] += 16
        tok = (s, r["val"][i], "dma")

        def thunk(e, waits=waits, s=s, out=out, in_=in_, kw=kw):
            for (ss, v) in waits:
                e.wait_ge(ss, v)
            src = in_(e) if callable(in_) else in_
            e.dma_start(out=out, in_=src, **kw).then_inc(s, 16)
        self.q[eng].append(thunk)
        self.ninst += 1
        self._commit(tok, reads, writes)
        return tok

    def wait_all(self, eng, toks):
        waits = self._waits(eng, list(toks))

        def thunk(e, waits=waits):
            for (s, v) in waits:
                e.wait_ge(s, v)
        self.q[eng].append(thunk)

    def flush(self):
        nc = self.nc
        qs = self.q
        self.q = {e: [] for e in ENGS}
        with nc.Block() as block:
            for e in ENGS:
                if not qs[e]:
                    continue

                def body(engine, lst=qs[e]):
                    for th in lst:
                        th(engine)
                getattr(block, e)(body)


D = 1024
INW = 2480
FH = 2816
EPS = 1e-6
O_MQ, O_MKV, O_MKR = 0, 256, 384
O_GQ, O_GK, O_GV, O_GA, O_GG = 416, 544, 672, 928, 944
O_BQ, O_BK, O_BV = 1200, 1456, 1712
O_RX, O_RG = 1968, 2224
NEGBIG = -30000.0


def vw(t, p0, np_, off, *dims):
    F = 1
    for s in t.shape[1:]:
        F *= s
    return bass.AP(t, p0 * F + off, [[F, np_]] + [[a, b] for (a, b) in dims])


class Ring:
    def __init__(self, items):
        self.items = list(items)
        self.i = 0

    def next(self):
        x = self.items[self.i % len(self.items)]
        self.i += 1
        return x


class Ctx:
    n = 0

    def __init__(self, nc, P):
        from contextlib import ExitStack
        self.nc, self.P = nc, P
        self.st = ExitStack()

    def sb(self, name, shape, dt):
        Ctx.n += 1
        return Buf(name, self.st.enter_context(self.nc.sbuf_tensor(f"{name}_{self.n}", list(shape), dt)))

    def ps(self, name, dt=F32):
        Ctx.n += 1
        shape = [128, 512] if dt == F32 else [128, 1024]
        return Buf(name, self.st.enter_context(self.nc.psum_tensor(f"{name}_{self.n}", shape, dt)), excl=True)

    def close(self):
        self.st.close()


def mm_group(P, bank, mms, reads):
    n = len(mms)
    for i, (o, l, r) in enumerate(mms):
        first, last = (i == 0), (i == n - 1)

        def fn(e, o=o, l=l, r=r, first=first, last=last):
            return e.matmul(o, lhsT=l, rhs=r, start=first, stop=last)
        P.op("tensor", fn, reads=reads if (first or last) else (), writes=[bank] if (first or last) else (),
             signal=last)


def mm1(P, bank, o, l, r, reads, start=True, stop=True, skip=False):
    def fn(e):
        return e.matmul(o, lhsT=l, rhs=r, start=start, stop=stop, skip_group_check=skip)
    P.op("tensor", fn, reads=reads, writes=[bank])


def drain_dma(P):
    toks = []
    for e, r in P.ring.items():
        toks += [(s, v, "dma") for s, v in zip(r["sems"], r["val"]) if v > 0]
    P.wait_all("sync", toks)


def rms_rstd(P, ss, rt, rstd, n_scale, bufs_r=()):
    P.op("scalar", lambda e: e.activation(out=rt[:], in_=ss[:], func=AF.Sqrt, scale=n_scale, bias=EPS),
         reads=[ss], writes=[rt])
    P.op("vector", lambda e: e.reciprocal(rstd[:], rt[:]), reads=[rt], writes=[rstd])


def mm_group(P, bank, mms, reads):
    n = len(mms)
    for i, (o, l, r) in enumerate(mms):
        first, last = (i == 0), (i == n - 1)
        fl = first or last
        P.I("tensor", "matmul", reads if fl else (), [bank] if fl else (), o, lhsT=l, rhs=r, start=first, stop=last,
            signal=last)


def mm1(P, bank, o, l, r, reads, start=True, stop=True, skip=False):
    P.I("tensor", "matmul", reads, [bank], o, lhsT=l, rhs=r, start=start, stop=stop, skip_group_check=skip)


def run_chains(P, gens, what):
    alive = list(gens)
    idle = 0
    while alive:
        n0 = P.ninst
        for g in list(alive):
            try:
                r = next(g)
                if r is not None:
                    alive.append(r)
            except StopIteration:
                alive.remove(g)
        idle = idle + 1 if P.ninst == n0 else 0
        assert idle < 1000, f"{what}: scheduler deadlock"


def run_chains(P, gens, what):
    alive = list(gens)
    idle = 0
    while alive:
        n0 = P.ninst
        for g in list(alive):
            try:
                r = next(g)
                if r is not None:
                    alive.append(r)
            except StopIteration:
                alive.remove(g)
        idle = idle + 1 if P.ninst == n0 else 0
        assert idle < 1000, f"{what}: scheduler deadlock"


def load_win(P, win, dr, l, INW):
    for c in range(8):
        P.dma("gpsimd", win[:, c, :], bass.AP(dr["w_in"], l * D * INW + c * 128 * INW, [[INW, 128], [1, INW]]), writes=[win])


def load_wc(P, wout, wfi, dr, l):
    for c in range(8):
        P.dma("gpsimd", wout[:, c, :], bass.AP(dr["w_out"], l * D * D + c * 128 * D, [[D, 128], [1, D]]), writes=[wout])
    for c in range(8):
        P.dma("gpsimd", wfi[:, c, :], bass.AP(dr["w_ffn_in"], l * D * 2 * FH + c * 128 * 2 * FH, [[2 * FH, 128], [1, 2 * FH]]),
              writes=[wfi])


def transposes(P, pt, items, reads):
    n = len(items)
    for i, (o, a) in enumerate(items):
        fl = (i == 0) or (i == n - 1)
        P.I("tensor", "transpose", reads if fl else (), [pt] if fl else (), o, a, P.ident[0:a.shape[0], 0:a.shape[0]],
            signal=(i == n - 1))


def rms_rstd(P, ss, rt, rstd, n_scale):
    P.I("scalar", "activation", [ss], [rt], out=rt[:], in_=ss[:], func=AF.Sqrt, scale=n_scale, bias=EPS)
    P.I("vector", "reciprocal", [rt], [rstd], rstd[:], rt[:])


def phase_c(nc, P, S, l, dr, ntok, last, dyn=False, wpre=None):
    C = Ctx(nc, P)
    TT = 256
    nt = ntok // TT
    if wpre is None:
        wout = C.sb("wout", [128, 8, 1024], BF16)
        wfi = C.sb("wfi", [128, 8, 2 * FH], BF16)
    else:
        wout, wfi = wpre
    wfo = C.sb("wfo", [128, 22, 1024], BF16)
    gF = C.sb("gF", [128, 1024], F32)
    gL = C.sb("gL", [128, 1024], F32) if last else None
    ident = C.sb("ident", [128, 128], BF16)
    P.ident = ident
    xs = [C.sb(f"xs{i}", [128, 2, 1024], F32) for i in range(2)]
    yT = [C.sb(f"yT{i}", [128, 8, TT], BF16) for i in range(2)]
    h2 = C.sb("h2", [128, 1024], BF16)
    h2Ts = [C.sb(f"h2T{i}", [128, 8, TT], BF16) for i in range(2)]
    actT = C.sb("actT", [128, 22, TT], BF16)
    sgr = Ring([C.sb(f"sg{i}", [128, TT], F32) for i in range(2)])
    junk = C.sb("junk", [128, 1024], BF16)
    ss = C.sb("ss", [128, 2], F32)
    rt = C.sb("rt", [128, 2], F32)
    rstd = C.sb("rstd", [128, 2], F32)
    ss2 = C.sb("ss2", [128, 2], F32)
    rt2 = C.sb("rt2", [128, 2], F32)
    rstd2 = C.sb("rstd2", [128, 2], F32)
    psf = Ring([C.ps(f"pf{i}") for i in range(6)])
    psb = Ring([C.ps(f"pb{i}", BF16) for i in range(2)])

    w_out, w_fi, w_fo = dr["w_out"], dr["w_ffn_in"], dr["w_ffn_out"]
    P.dma("sync", ident[:], dr["c_ident"].ap(), writes=[ident])
    P.dma("sync", gF[:], bass.AP(dr["ffn_norm"], l * D, [[0, 128], [1, D]]), writes=[gF])
    if last:
        P.dma("sync", gL[:], bass.AP(dr["final_norm"], 0, [[0, 128], [1, D]]), writes=[gL])

    if dyn:
        roff_sb = C.sb("roff_sb", [1, 1], mybir.dt.int32)
        tokd = P.dma("sync", roff_sb[:], dr["roff"].ap(), writes=[roff_sb])

        def setreg(e, tokd=tokd):
            e.wait_ge(tokd[0], tokd[1])
            e.reg_load(P.sync_reg, roff_sb[:1, :1])
            P.roff_val = e.snap(P.sync_reg)
        P.raw("sync", setreg)
    yTv = dr["yT"].ap().rearrange("(c p) s -> p c s", p=128)

    def load(u):
        b = u % 2
        r0 = u * TT
        P.dma("sync", xs[b][:], bass.AP(dr["xc_in"], r0 * D, [[D, 128], [128 * D, 2], [1, D]]), writes=[xs[b]])
        if dyn:
            P.dma("sync", yT[b][:], lambda e, r0=r0: yTv[:, :, bass.ds(P.roff_val + r0, TT)], writes=[yT[b]])
        else:
            P.dma("sync", yT[b][:], bass.AP(dr["yT"], r0, [[S, 128], [128 * S, 8], [1, TT]]), writes=[yT[b]])

    if wpre is None:
        load_wc(P, wout, wfi, dr, l)
    for j in range(22):
        P.dma("gpsimd", wfo[:, j, :], bass.AP(w_fo, l * FH * D + j * 128 * D, [[D, 128], [1, D]]), writes=[wfo])

    class BankPool:
        def __init__(self, banks):
            self.free = list(banks)

        def get(self):
            while not self.free:
                yield
            return self.free.pop(0)

        def put(self, b_):
            self.free.append(b_)

    FP = BankPool(psf.items)
    BP = BankPool(psb.items)
    front_done = [0]
    ffn_done = [0]

    def front_chain():
        for u in range(nt):
            b = u % 2
            while loaded[0] <= u:
                yield
            if u >= 2:
                for _ in range(16):
                    yield
            X, Y, H2T = xs[b], yT[b], h2Ts[b]
            for s_ in range(2):
                for n in range(2):
                    pb = yield from FP.get()
                    mm_group(P, pb, [(pb[:, :], Y[:, c, s_ * 128:(s_ + 1) * 128], wout[:, c, n * 512:(n + 1) * 512])
                                     for c in range(8)], reads=[Y, wout])
                    xsl = X[:, s_, n * 512:(n + 1) * 512]
                    P.I("vector", "tensor_tensor", [pb, X], [X], out=xsl, in0=pb[:, :], in1=xsl, op=ALU.add)
                    FP.put(pb)
                    yield
            for s_ in range(2):
                P.I("scalar", "activation", [X], [junk, ss], out=junk[:], in_=X[:, s_, :], func=AF.Square,
                    scale=float(D ** -0.5), accum_out=ss[:, s_:s_ + 1])
                yield
            rms_rstd(P, ss, rt, rstd, 1.0)
            yield
            for s_ in range(2):
                P.I("vector", "scalar_tensor_tensor", [X, rstd, gF], [h2], out=h2[:], in0=X[:, s_, :],
                    scalar=rstd[:, s_:s_ + 1], in1=gF[:], op0=ALU.mult, op1=ALU.mult)
                yield
                pt = yield from BP.get()
                transposes(P, pt, [(pt[:, c * 128:(c + 1) * 128], h2[:, c * 128:(c + 1) * 128]) for c in range(8)], [h2, ident])
                P.I("scalar", "copy", [pt], [H2T], out=H2T[:, :, s_ * 128:(s_ + 1) * 128],
                    in_=vw(pt.t, 0, 128, 0, (128, 8), (1, 128)))
                BP.put(pt)
                yield
            front_done[0] = u + 1
            yield

    loaded = [0]

    def load_chain():
        for u in range(nt):
            while u >= 2 and ffn_done[0] < u - 1:
                yield
            load(u)
            loaded[0] = u + 1
            yield

    def ffn_chain():
        for u in range(nt):
            b = u % 2
            while front_done[0] <= u:
                yield
            X, H2T = xs[b], h2Ts[b]
            for j in range(22):
                pb = yield from FP.get()
                mm_group(P, pb, [(pb[:, 0:TT], wfi[:, c, j * 128:(j + 1) * 128], H2T[:, c, :]) for c in range(8)],
                         reads=[wfi, H2T])
                yield
                mm_group(P, pb, [(pb[:, TT:2 * TT], wfi[:, c, FH + j * 128:FH + (j + 1) * 128], H2T[:, c, :])
                                 for c in range(8)], reads=[wfi, H2T])
                g_ = sgr.next()
                P.I("scalar", "activation", [pb], [g_], out=g_[:], in_=pb[:, 0:TT], func=AF.Silu)
                P.I("vector", "tensor_tensor", [pb, g_], [actT], out=actT[:, j, :], in0=pb[:, TT:2 * TT], in1=g_[:], op=ALU.mult)
                FP.put(pb)
                yield
            for s_ in range(2):
                for n in range(2):
                    pb = yield from FP.get()
                    mm_group(P, pb, [(pb[:, :], actT[:, j, s_ * 128:(s_ + 1) * 128], wfo[:, j, n * 512:(n + 1) * 512])
                                     for j in range(22)], reads=[actT, wfo])
                    xsl = X[:, s_, n * 512:(n + 1) * 512]
                    P.I("vector", "tensor_tensor", [pb, X], [X], out=xsl, in0=pb[:, :], in1=xsl, op=ALU.add)
                    FP.put(pb)
                    yield
            if last:
                for s_ in range(2):
                    P.I("scalar", "activation", [X], [junk, ss2], out=junk[:], in_=X[:, s_, :], func=AF.Square,
                        scale=float(D ** -0.5), accum_out=ss2[:, s_:s_ + 1])
                rms_rstd(P, ss2, rt2, rstd2, 1.0)
                yield
                for s_ in range(2):
                    P.I("vector", "scalar_tensor_tensor", [X, rstd2, gL], [X], out=X[:, s_, :], in0=X[:, s_, :],
                        scalar=rstd2[:, s_:s_ + 1], in1=gL[:], op0=ALU.mult, op1=ALU.mult)
                yield
            r0 = u * TT
            P.dma("gpsimd", bass.AP(dr["xout"], r0 * D, [[D, 128], [128 * D, 2], [1, D]]), X[:], reads=[X])
            ffn_done[0] = u + 1
            yield

    run_chains(P, [load_chain(), front_chain(), ffn_chain()], "phase C")
    drain_dma(P)
    P.flush()
    C.close()


def phase_a(nc, P, S, l, dr, NH=4, win=None):
    roles = (0, 1)
    C = Ctx(nc, P)
    NT = S // 128
    NJ = NH // 2
    NMLA, NMOBA = 2, 2
    F = NH * 32
    V4 = NH * 64
    O_MQ, O_MKV, O_MKR = 0, 256, 384
    O_GQ = 416
    O_GK = O_GQ + F
    O_GV = O_GK + F
    O_GA = O_GV + V4
    O_GG = O_GA + 16
    O_BQ = O_GG + V4
    O_BK = O_BQ + V4
    O_BV = O_BK + V4
    O_RX = O_BV + V4
    O_RG = O_RX + NJ * 128
    INW = O_RG + NJ * 128
    I = P.I
    ident = C.sb("ident", [128, 128], BF16)
    P.ident = ident
    win_preloaded = win is not None
    if win is None:
        win = C.sb("win", [128, 8, INW], BF16)
    gA = C.sb("gA", [128, 1024], F32)
    xs = [C.sb(f"xs{i}", [128, 1024], F32) for i in range(2)]
    h = C.sb("h", [128, 1024], BF16)
    hT = [C.sb(f"hT{i}", [128, 8, 128], BF16) for i in range(3)]
    junk = C.sb("junk", [128, 1024], BF16)
    ss = C.sb("ss", [128, 4], F32)
    rt = C.sb("rt", [128, 4], F32)
    rstd = C.sb("rstd", [128, 4], F32)
    psf = Ring([C.ps(f"pf{i}") for i in range(5)])
    psb = Ring([C.ps(f"pb{i}", BF16) for i in range(3)])
    evac = Ring(["scalar", "vector"])

    taps = dr.setdefault("_taps", []) if "dbg" in dr else None

    def tap(name, buf, ap, np_, ncol):
        if taps is None:
            return
        i = len(taps)
        taps.append((name, np_, ncol))
        P.dma("gpsimd", bass.AP(dr["dbg"], i * 128 * 512, [[512, np_], [1, ncol]]), ap, reads=[buf])

    def copy(eng, reads, writes, out, in_):
        if eng == "scalar":
            I("scalar", "copy", reads, writes, out=out, in_=in_)
        else:
            I(eng, "tensor_copy", reads, writes, out, in_)

    P.dma("sync", ident[:], dr["c_ident"].ap(), writes=[ident])
    P.dma("sync", gA[:], bass.AP(dr["attn_norm"], l * D, [[0, 128], [1, D]]), writes=[gA])

    xrow = dr.get("xin_rowmap") or (lambda t0: t0)

    def load(t):
        P.dma("sync", xs[t % 2][:], bass.AP(dr["xin"], xrow(t * 128) * D, [[D, 128], [1, D]]), writes=[xs[t % 2]])

    load(0)
    if not win_preloaded:
        load_win(P, win, dr, l, INW)

    if 0 in roles:
        wuq = C.sb("wuq", [128, 2, NH * 96], BF16)
        wukv = C.sb("wukv", [128, NH * 128], BF16)
        gq = C.sb("gq", [128, 256], F32)
        gkv = C.sb("gkv", [128, 128], F32)
        cosA = C.sb("cosA", [128, NT, 16], F32)
        sinA = C.sb("sinA", [128, NT, 16], F32)
        MB = []
        for inst_ in range(NMLA):
            cqn = C.sb("cqn", [128, 256], BF16)
            ckvn = C.sb("ckvn", [128, 128], BF16)
            cqnT = C.sb("cqnT", [128, 2, 128], BF16)
            ckvnT = C.sb("ckvnT", [128, 128], BF16)
            kpe = C.sb("kpe", [128, 32], F32)
            tmp = [C.sb(f"tmp{i}", [128, NH * 16], F32) for i in range(4)]
            qtm = C.sb("qtm", [128, NH, 96], BF16)
            ktm = C.sb("ktm", [128, NH, 96], BF16)
            vtm = [C.sb(f"vtm{i}", [128, NH, 65], BF16) for i in range(2)]
            qTs = [C.sb(f"qTs{i}", [96, NH, 128], BF16) for i in range(2)]
            kTs = [C.sb(f"kTs{i}", [96, NH, 128], BF16) for i in range(2)]
            junk2 = C.sb("junk2", [128, 256], BF16)
            ss2 = C.sb("ss2", [128, 4], F32)
            rt2 = C.sb("rt2", [128, 4], F32)
            rstd2 = C.sb("rstd2", [128, 4], F32)
            for v_ in vtm:
                I("gpsimd", "memset", [], [v_], v_[:], 1.0)
            MB.append(dict(cqn=cqn, ckvn=ckvn, cqnT=cqnT, ckvnT=ckvnT, kpe=kpe, tmp=tmp, qtm=qtm, ktm=ktm, vtm=vtm, qTs=qTs, kTs=kTs, junk2=junk2, ss2=ss2, rt2=rt2, rstd2=rstd2))
        for c in range(2):
            P.dma("gpsimd", wuq[:, c, :], bass.AP(dr["mla_w_uq"], l * 256 * NH * 96 + c * 128 * NH * 96, [[NH * 96, 128], [1, NH * 96]]), writes=[wuq])
        P.dma("gpsimd", wukv[:], bass.AP(dr["mla_w_ukv"], l * 128 * NH * 128, [[NH * 128, 128], [1, NH * 128]]), writes=[wukv])
        P.dma("sync", gq[:], bass.AP(dr["mla_q_norm"], l * 256, [[0, 128], [1, 256]]), writes=[gq])
        P.dma("sync", gkv[:], bass.AP(dr["mla_kv_norm"], l * 128, [[0, 128], [1, 128]]), writes=[gkv])
        P.dma("sync", cosA[:], bass.AP(dr["c_cosA"], 0, [[16, 128], [128 * 16, NT], [1, 16]]), writes=[cosA])
        P.dma("sync", sinA[:], bass.AP(dr["c_sinA"], 0, [[16, 128], [128 * 16, NT], [1, 16]]), writes=[sinA])
        wa2 = C.sb("wa2", [17, F], BF16)
        ghn = C.sb("ghn", [128, 64], F32)
        cum = C.sb("cum", [128, 128], F32)
        chones = C.sb("chones", [128, 128], F32)
        chind = C.sb("chind", [128, 2], F32)
        gmask = C.sb("gmask", [128, 128], BF16)
        bdm = C.sb("bdm", [128, 256], F32)
        Sf = Ring([C.sb(f"Sf{i}", [128, V4], F32) for i in range(6)])
        Sb = Ring([C.sb(f"Sb{i}", [128, V4], BF16) for i in range(6)])
        GB = []
        for inst_ in range(2):
            alow = C.sb("alow", [128, 16], BF16)
            alT = [C.sb(f"alT{i}", [17, 128], BF16) for i in range(2)]
            e1 = C.sb("e1", [128, F], F32)
            sp = C.sb("sp", [128, F], F32)
            cbs = C.sb("cbs", [128, F], F32)
            d1 = C.sb("d1", [128, F], F32)
            d2 = C.sb("d2", [128, F], F32)
            E = [C.sb(f"E{i}", [128, F], F32) for i in range(4)]
            dec = C.sb("dec", [F, 2], F32)
            qt_ = C.sb("qt_", [128, F], BF16)
            qh_ = C.sb("qh_", [128, F], BF16)
            kt_ = C.sb("kt_", [128, F], BF16)
            kh_ = C.sb("kh_", [128, F], BF16)
            vbf = C.sb("vbf", [128, V4], BF16)
            qtT = C.sb("qtT", [32, NH, 128], BF16)
            ktT = C.sb("ktT", [32, NH, 128], BF16)
            qhT = C.sb("qhT", [F, 128], BF16)
            attm = C.sb("attm", [128, NH, 128], BF16)
            tmpS = C.sb("tmpS", [128, V4], F32)
            sq = C.sb("sq", [128, V4], F32)
            ss4 = C.sb("ss4", [128, NH], F32)
            rt4 = C.sb("rt4", [128, NH], F32)
            rstd4 = C.sb("rstd4", [128, NH], F32)
            on = C.sb("on", [128, V4], F32)
            on2 = C.sb("on2", [128, V4], F32)
            sgl = C.sb("sgl", [128, V4], F32)
            yb = C.sb("yb", [128, V4], BF16)
            ybT = [C.sb(f"ybT{i}", [128, NJ, 128], BF16) for i in range(2)]
            GB.append(dict(alow=alow, alT=alT, e1=e1, sp=sp, cbs=cbs, d1=d1, d2=d2, E=E, dec=dec, qt_=qt_, qh_=qh_, kt_=kt_, kh_=kh_, vbf=vbf, qtT=qtT, ktT=ktT, qhT=qhT, attm=attm, tmpS=tmpS, sq=sq, ss4=ss4, rt4=rt4, rstd4=rstd4, on=on, on2=on2, sgl=sgl, yb=yb, ybT=ybT))
        P.dma("gpsimd", wa2[0:16, :], bass.AP(dr["gla_w_a2"], l * 16 * F, [[F, 16], [1, F]]), writes=[wa2])
        P.dma("gpsimd", wa2[16:17, :], bass.AP(dr["gla_b_a2"], l * F, [[F, 1], [1, F]]), writes=[wa2])
        P.dma("sync", ghn[:], bass.AP(dr["gla_head_norm"], l * 64, [[0, 128], [1, 64]]), writes=[ghn])
        P.dma("sync", cum[:], dr["c_cum"].ap(), writes=[cum])
        P.dma("sync", chones[:], dr["c_chones"].ap(), writes=[chones])
        P.dma("sync", chind[:], dr["c_chind"].ap(), writes=[chind])
        P.dma("sync", gmask[:], dr["c_gmask"].ap(), writes=[gmask])
        P.dma("sync", bdm[:], dr["c_bdm"].ap(), writes=[bdm])
        for G_ in GB:
            for a_ in G_["alT"]:
                I("gpsimd", "memset", [], [a_], a_[:], 1.0)
        Sa_f, Sa_b = Sf.next(), Sb.next()
        I("gpsimd", "memset", [], [Sa_f], Sa_f[:], 0.0)
        I("gpsimd", "memset", [], [Sa_b], Sa_b[:], 0.0)

    if 1 in roles:
        cosC = C.sb("cosC", [128, NT, 8], F32)
        sinC = C.sb("sinC", [128, NT, 8], F32)
        OB = []
        for inst_ in range(NMOBA):
            qktm = C.sb("qktm", [128, 2 * NH, 64], BF16)
            rtmp = [C.sb(f"rtmp{i}", [128, 2 * NH, 8], F32) for i in range(4)]
            vtc = [C.sb(f"vtc{i}", [128, NH, 65], BF16) for i in range(2)]
            qkTs = [C.sb(f"qkTs{i}", [64, 2 * NH, 128], BF16) for i in range(2)]
            for v_ in vtc:
                I("gpsimd", "memset", [], [v_], v_[:], 1.0)
            OB.append(dict(qktm=qktm, rtmp=rtmp, vtc=vtc, qkTs=qkTs))
        P.dma("sync", cosC[:], bass.AP(dr["c_cosC"], 0, [[8, 128], [128 * 8, NT], [1, 8]]), writes=[cosC])
        P.dma("sync", sinC[:], bass.AP(dr["c_sinC"], 0, [[8, 128], [128 * 8, NT], [1, 8]]), writes=[sinC])
        cw = C.sb("cw", [128, 4, NJ], F32)
        cbv = C.sb("cbv", [128, NJ], F32)
        bab = C.sb("bab", [128, NJ], F32)
        bxb = C.sb("bxb", [128, NJ], F32)
        lam = C.sb("lam", [128, NJ], F32)
        lt = [C.sb(f"lt{i}", [128, NJ], F32) for i in range(4)]
        cc = C.sb("cc", [128, NJ], F32)
        BDa = C.sb("BDa", [128, NJ, 128], BF16)
        BDx = C.sb("BDx", [128, NJ, 128], BF16)
        xpad = [C.sb(f"xpad{i}", [128, NJ, 131], F32) for i in range(2)]
        xc = C.sb("xc", [128, NJ, 128], F32)
        xcb = C.sb("xcb", [128, NJ, 128], BF16)
        r_ = C.sb("r_", [128, NJ, 128], F32)
        i_ = C.sb("i_", [128, NJ, 128], F32)
        a_ = C.sb("a_", [128, NJ, 128], F32)
        a2 = C.sb("a2", [128, NJ, 128], F32)
        ug = C.sb("ug", [128, NJ, 128], F32)
        m_ = C.sb("m_", [128, NJ, 128], F32)
        u_ = C.sb("u_", [128, NJ, 128], F32)
        hs = [C.sb(f"hs{i}", [128, NJ, 128], F32) for i in range(2)]
        gl = C.sb("gl", [128, NJ, 128], F32)
        yd = [C.sb(f"yd{i}", [128, NJ, 128], BF16) for i in range(2)]
        for k in range(4):
            for j in range(NJ):
                P.dma("sync", cw[:, k, j:j + 1], bass.AP(dr["lru_conv_w"], (l * 4 + k) * NJ * 128 + j * 128, [[1, 128], [1, 1]]),
                      writes=[cw])
        for (tile_, name) in ((cbv, "lru_conv_b"), (bab, "lru_b_a"), (bxb, "lru_b_x"), (lam, "lru_lambda")):
            for j in range(NJ):
                P.dma("sync", tile_[:, j:j + 1], bass.AP(dr[name], l * NJ * 128 + j * 128, [[1, 128], [1, 1]]), writes=[tile_])
        I("gpsimd", "memset", [], [BDa], BDa[:], 0.0)
        I("gpsimd", "memset", [], [BDx], BDx[:], 0.0)
        for (bd, name) in ((BDa, "lru_w_a"), (BDx, "lru_w_x")):
            for n in range(2 * NJ):
                j, q = n // 2, n % 2
                P.dma("gpsimd", bd[q * 64:(q + 1) * 64, j, q * 64:(q + 1) * 64],
                      bass.AP(dr[name], (l * 2 * NJ + n) * 64 * 64, [[64, 64], [1, 64]]), writes=[bd])
        al, ee, s_, acc = lt
        I("vector", "tensor_scalar_mul", [lam], [al], al[:], lam[:], -1.0)
        I("vector", "tensor_tensor", [lam, al], [al], out=al[:], in0=al[:], in1=lam[:], op=ALU.max)
        I("scalar", "activation", [al], [ee], out=ee[:], in_=al[:], func=AF.Exp, scale=-1.0)
        I("vector", "tensor_scalar_add", [ee], [s_], s_[:], ee[:], 2.0)
        I("vector", "reciprocal", [s_], [s_], s_[:], s_[:])
        I("vector", "tensor_tensor", [s_, ee], [s_], out=s_[:], in0=s_[:], in1=ee[:], op=ALU.mult)
        I("vector", "tensor_tensor", [s_], [ee], out=ee[:], in0=s_[:], in1=s_[:], op=ALU.mult)
        I("vector", "memset", [], [acc], acc[:], 1.0 / 17.0)
        for k in (15, 13, 11, 9, 7, 5, 3, 1):
            I("vector", "tensor_tensor", [acc, ee], [acc], out=acc[:], in0=acc[:], in1=ee[:], op=ALU.mult)
            I("vector", "tensor_scalar_add", [acc], [acc], acc[:], acc[:], 1.0 / k)
        I("vector", "tensor_tensor", [acc, s_], [acc], out=acc[:], in0=acc[:], in1=s_[:], op=ALU.mult)
        I("vector", "tensor_tensor", [al, lam], [al], out=al[:], in0=al[:], in1=lam[:], op=ALU.subtract)
        I("vector", "tensor_scalar", [al], [al], out=al[:], in0=al[:], scalar1=-4.0, scalar2=None, op0=ALU.mult)
        I("vector", "scalar_tensor_tensor", [acc, al], [cc], out=cc[:], in0=acc[:], scalar=-16.0, in1=al[:],
          op0=ALU.mult, op1=ALU.add)
        I("gpsimd", "memset", [], [xpad[1]], xpad[1][:], 0.0)
        nbab = C.sb("nbab", [128, NJ], F32)
        nbxb = C.sb("nbxb", [128, NJ], F32)
        I("vector", "tensor_scalar_mul", [bab], [nbab], nbab[:], bab[:], -1.0)
        I("vector", "tensor_scalar_mul", [bxb], [nbxb], nbxb[:], bxb[:], -1.0)

    NHT = 3
    NCH = 4
    ht_ready = [0]
    ht_done = [0] * NT
    st = {}
    if 0 in roles:
        st['Sa_f'], st['Sa_b'] = Sa_f, Sa_b
        st['ready'] = 0

    class BankPool:
        def __init__(self, banks):
            self.free = list(banks)

        def get(self):
            while not self.free:
                yield
            return self.free.pop(0)

        def put(self, b_):
            self.free.append(b_)

    FP = BankPool(psf.items + [])
    BP = BankPool(psb.items + [])

    def proj_tm_(HT, bank, c0, n):
        mm_group(P, bank, [(bank[:, 0:n], HT[:, c, :], win[:, c, c0:c0 + n]) for c in range(8)], [HT, win])

    def norm_chain():
        for t in range(NT):
            b = t % 2
            while t >= NHT and ht_done[t - NHT] < NCH:
                yield
            if t + 1 < NT:
                load(t + 1)
            X = xs[b]
            HT = hT[t % NHT]
            I("scalar", "activation", [X], [junk, ss], out=junk[:], in_=X[:], func=AF.Square, scale=float(D ** -0.5),
              accum_out=ss[:, 0:1])
            yield
            I("scalar", "activation", [ss], [rt], out=rt[:, 0:1], in_=ss[:, 0:1], func=AF.Ln, bias=EPS, scale=1.0)
            yield
            I("scalar", "activation", [rt], [rstd], out=rstd[:, 0:1], in_=rt[:, 0:1], func=AF.Exp, scale=-0.5)
            yield
            I("vector", "scalar_tensor_tensor", [X, rstd, gA], [h], out=h[:], in0=X[:], scalar=rstd[:, 0:1], in1=gA[:],
              op0=ALU.mult, op1=ALU.mult)
            yield
            pt = (yield from BP.get())
            transposes(P, pt, [(pt[:, c * 128:(c + 1) * 128], h[:, c * 128:(c + 1) * 128]) for c in range(8)], [h, ident])
            yield
            copy("scalar", [pt], [HT], HT[:], vw(pt.t, 0, 128, 0, (128, 8), (1, 128)))
            yield
            BP.put(pt)


            ht_ready[0] = t + 1
            yield

    def mla_chain(inst):
        M_ = MB[inst]
        cqn = M_['cqn']
        ckvn = M_['ckvn']
        cqnT = M_['cqnT']
        ckvnT = M_['ckvnT']
        kpe = M_['kpe']
        tmp = M_['tmp']
        qtm = M_['qtm']
        ktm = M_['ktm']
        vtm = M_['vtm']
        qTs = M_['qTs']
        kTs = M_['kTs']
        junk2 = M_['junk2']
        ss2 = M_['ss2']
        rt2 = M_['rt2']
        rstd2 = M_['rstd2']
        for t in range(inst, NT, NMLA):
            b = (t // NMLA) % 2
            while ht_ready[0] <= t:
                yield
            HT = hT[t % NHT]
            pa = (yield from FP.get())
            proj_tm_(HT, pa, O_MQ, 416)
            yield
            ht_done[t] += 1
            I("scalar", "activation", [pa], [junk2, ss2], out=junk2[:, 0:256], in_=pa[:, 0:256], func=AF.Square,
              scale=1.0 / 16.0, accum_out=ss2[:, 1:2])
            yield
            I("scalar", "activation", [pa], [junk2, ss2], out=junk2[:, 0:128], in_=pa[:, 256:384], func=AF.Square,
              scale=float(128 ** -0.5), accum_out=ss2[:, 2:3])
            yield
            I("scalar", "activation", [ss2], [rt2], out=rt2[:, 1:3], in_=ss2[:, 1:3], func=AF.Ln, bias=EPS, scale=1.0)
            yield
            I("scalar", "activation", [rt2], [rstd2], out=rstd2[:, 1:3], in_=rt2[:, 1:3], func=AF.Exp, scale=-0.5)
            yield
            I("vector", "scalar_tensor_tensor", [pa, rstd2, gq], [cqn], out=cqn[:], in0=pa[:, 0:256], scalar=rstd2[:, 1:2],
              in1=gq[:], op0=ALU.mult, op1=ALU.mult)
            yield
            I("vector", "scalar_tensor_tensor", [pa, rstd2, gkv], [ckvn], out=ckvn[:], in0=pa[:, 256:384],
              scalar=rstd2[:, 2:3], in1=gkv[:], op0=ALU.mult, op1=ALU.mult)
            yield
            cs_, sn_ = cosA[:, t, :], sinA[:, t, :]
            I("vector", "tensor_tensor", [pa, cosA], [tmp[0]], out=tmp[0][:, 0:16], in0=pa[:, 384:400], in1=cs_, op=ALU.mult)
            yield
            I("vector", "tensor_tensor", [pa, sinA], [tmp[1]], out=tmp[1][:, 0:16], in0=pa[:, 400:416], in1=sn_, op=ALU.mult)
            yield
            I("vector", "tensor_tensor", [tmp[0], tmp[1]], [kpe], out=kpe[:, 0:16], in0=tmp[0][:, 0:16], in1=tmp[1][:, 0:16], op=ALU.subtract)
            yield
            I("vector", "tensor_tensor", [pa, cosA], [tmp[2]], out=tmp[2][:, 0:16], in0=pa[:, 400:416], in1=cs_, op=ALU.mult)
            yield
            I("vector", "tensor_tensor", [pa, sinA], [tmp[3]], out=tmp[3][:, 0:16], in0=pa[:, 384:400], in1=sn_, op=ALU.mult)
            yield
            I("vector", "tensor_tensor", [tmp[2], tmp[3]], [kpe], out=kpe[:, 16:32], in0=tmp[2][:, 0:16], in1=tmp[3][:, 0:16], op=ALU.add)
            yield
            FP.put(pa)
            I("gpsimd", "tensor_copy", [kpe], [ktm], ktm[:, :, 64:96], vw(kpe.t, 0, 128, 0, (0, NH), (1, 32)))
            yield
            pt = (yield from BP.get())
            transposes(P, pt, [(pt[:, 0:128], cqn[:, 0:128]), (pt[:, 128:256], cqn[:, 128:256]), (pt[:, 256:384], ckvn[:])],
                       [cqn, ckvn, ident])
            yield
            copy("vector", [pt], [cqnT], cqnT[:], vw(pt.t, 0, 128, 0, (128, 2), (1, 128)))
            yield
            copy("scalar", [pt], [ckvnT], ckvnT[:], pt[:, 256:384])
            yield
            BP.put(pt)
            pq = (yield from FP.get())
            mm_group(P, pq, [(pq[:, 0:NH * 96], cqnT[:, c, :], wuq[:, c, :]) for c in range(2)], [cqnT, wuq])
            yield
            pkv = (yield from FP.get())
            mm_group(P, pkv, [(pkv[:, 0:NH * 128], ckvnT[:], wukv[:])], [ckvnT, wukv])
            yield
            copy("scalar", [pq], [qtm], qtm[:, :, 0:64], vw(pq.t, 0, 128, 0, (96, NH), (1, 64)))
            yield
            x1 = vw(pq.t, 0, 128, 64, (96, NH), (1, 16))
            x2 = vw(pq.t, 0, 128, 80, (96, NH), (1, 16))
            cs4 = vw(cosA.t, 0, 128, t * 16, (0, NH), (1, 16))
            sn4 = vw(sinA.t, 0, 128, t * 16, (0, NH), (1, 16))
            tv = [vw(x.t, 0, 128, 0, (16, NH), (1, 16)) for x in tmp]
            I("vector", "tensor_tensor", [pq, cosA], [tmp[0]], out=tv[0], in0=x1, in1=cs4, op=ALU.mult)
            yield
            I("vector", "tensor_tensor", [pq, sinA], [tmp[1]], out=tv[1], in0=x2, in1=sn4, op=ALU.mult)
            yield
            I("vector", "tensor_tensor", [tmp[0], tmp[1]], [qtm], out=qtm[:, :, 64:80], in0=tv[0], in1=tv[1], op=ALU.subtract)
            yield
            I("vector", "tensor_tensor", [pq, cosA], [tmp[2]], out=tv[2], in0=x2, in1=cs4, op=ALU.mult)
            yield
            I("vector", "tensor_tensor", [pq, sinA], [tmp[3]], out=tv[3], in0=x1, in1=sn4, op=ALU.mult)
            yield
            I("vector", "tensor_tensor", [tmp[2], tmp[3]], [qtm], out=qtm[:, :, 80:96], in0=tv[2], in1=tv[3], op=ALU.add)
            yield
            FP.put(pq)
            copy("scalar", [pkv], [ktm], ktm[:, :, 0:64], vw(pkv.t, 0, 128, 0, (128, NH), (1, 64)))
            yield
            VT = vtm[b]
            copy("vector", [pkv], [VT], VT[:, :, 0:64], vw(pkv.t, 0, 128, 64, (128, NH), (1, 64)))
            yield
            FP.put(pkv)
            P.dma("sync", bass.AP(dr["v_a"], t * 128 * NH * 65, [[NH * 65, 128], [1, NH * 65]]), VT[:], reads=[VT])
            yield
            for (src, dstT, dname) in ((qtm, qTs[b], "qT_a"), (ktm, kTs[b], "kT_a")):
                pt = (yield from BP.get())
                transposes(P, pt, [(pt[0:96, hh * 128:(hh + 1) * 128], src[:, hh, :]) for hh in range(NH)], [src, ident])
                yield
                copy(evac.next(), [pt], [dstT], dstT[:], vw(pt.t, 0, 96, 0, (128, NH), (1, 128)))
                yield
                BP.put(pt)
                P.dma("sync", bass.AP(dr[dname], t * 128, [[S, 96], [96 * S, NH], [1, 128]]), dstT[:], reads=[dstT])
                yield


    def gla_chain(inst):
        G_ = GB[inst]
        alow = G_['alow']
        alT = G_['alT']
        e1 = G_['e1']
        sp = G_['sp']
        cbs = G_['cbs']
        d1 = G_['d1']
        d2 = G_['d2']
        E = G_['E']
        dec = G_['dec']
        qt_ = G_['qt_']
        qh_ = G_['qh_']
        kt_ = G_['kt_']
        kh_ = G_['kh_']
        vbf = G_['vbf']
        qtT = G_['qtT']
        ktT = G_['ktT']
        qhT = G_['qhT']
        attm = G_['attm']
        tmpS = G_['tmpS']
        sq = G_['sq']
        ss4 = G_['ss4']
        rt4 = G_['rt4']
        rstd4 = G_['rstd4']
        on = G_['on']
        on2 = G_['on2']
        sgl = G_['sgl']
        yb = G_['yb']
        ybT = G_['ybT']
        for t in range(inst, NT, 2):
            b = (t // 2) % 2
            while ht_ready[0] <= t:
                yield
            HT = hT[t % NHT]
            pg1 = (yield from FP.get())
            proj_tm_(HT, pg1, O_GQ, 2 * F + V4)
            yield
            pg2 = (yield from FP.get())
            proj_tm_(HT, pg2, O_GA, 16 + V4)
            yield
            ht_done[t] += 1
            copy("vector", [pg2], [alow], alow[:], pg2[:, 0:16])
            yield
            I("scalar", "activation", [pg2], [sgl], out=sgl[:], in_=pg2[:, 16:16 + V4], func=AF.Exp, scale=-1.0)
            yield
            I("scalar", "activation", [sgl], [sgl], out=sgl[:], in_=sgl[:], func=AF.Ln, bias=1.0, scale=1.0)
            yield
            I("scalar", "activation", [sgl], [sgl], out=sgl[:], in_=sgl[:], func=AF.Exp, scale=-1.0)
            yield
            I("vector", "tensor_tensor", [pg2, sgl], [sgl], out=sgl[:], in0=pg2[:, 16:16 + V4], in1=sgl[:], op=ALU.mult)
            yield
            FP.put(pg2)
            AT = alT[b]
            pt = (yield from BP.get())
            transposes(P, pt, [(pt[0:16, 0:128], alow[:])], [alow, ident])
            yield
            copy("vector", [pt], [AT], AT[0:16, :], pt[0:16, 0:128])
            yield
            BP.put(pt)
            pz = (yield from FP.get())
            mm1(P, pz, pz[:, 0:F], AT[:], wa2[:], [AT, wa2])
            yield
            I("scalar", "activation", [pz], [e1], out=e1[:], in_=pz[:, 0:F], func=AF.Exp, scale=-1.0)
            yield
            I("scalar", "activation", [e1], [sp], out=sp[:], in_=e1[:], func=AF.Ln, bias=1.0, scale=1.0)
            yield
            mm1(P, pz, pz[:, 128:128 + F], cum[:], sp[:], [cum, sp])
            yield
            mm1(P, pz, pz[:, 256:256 + F], chones[:], sp[:], [chones, sp])
            yield
            mm1(P, pz, pz[0:F, 384:386], sp[:], chind[:], [chind, sp])
            yield
            I("scalar", "activation", [pz], [dec], out=dec[:], in_=pz[0:F, 384:386], func=AF.Exp, scale=-1.0 / 16.0)
            yield
            copy("scalar", [pz], [cbs], cbs[:], pz[:, 128:128 + F])
            yield
            I("vector", "scalar_tensor_tensor", [pz, cbs], [d1], out=d1[:], in0=pz[:, 256:256 + F], scalar=0.5, in1=cbs[:],
              op0=ALU.mult, op1=ALU.subtract)
            yield
            I("vector", "scalar_tensor_tensor", [pz, cbs], [d2], out=d2[:], in0=pz[:, 256:256 + F], scalar=-1.0, in1=cbs[:],
              op0=ALU.mult, op1=ALU.add)
            yield
            FP.put(pz)
            I("scalar", "activation", [d1], [E[0]], out=E[0][:], in_=d1[:], func=AF.Exp, scale=1.0 / 16.0)
            yield
            I("scalar", "activation", [d1], [E[1]], out=E[1][:], in_=d1[:], func=AF.Exp, scale=-1.0 / 16.0)
            yield
            I("scalar", "activation", [d2], [E[2]], out=E[2][:], in_=d2[:], func=AF.Exp, scale=1.0 / 16.0)
            yield
            I("scalar", "activation", [cbs], [E[3]], out=E[3][:], in_=cbs[:], func=AF.Exp, scale=-1.0 / 16.0)
            yield
            qsc = float(32 ** -0.5)
            I("vector", "scalar_tensor_tensor", [pg1, E[0]], [qt_], out=qt_[:], in0=pg1[:, 0:F], scalar=qsc, in1=E[0][:],
              op0=ALU.mult, op1=ALU.mult)
            yield
            I("vector", "scalar_tensor_tensor", [pg1, E[3]], [qh_], out=qh_[:], in0=pg1[:, 0:F], scalar=qsc, in1=E[3][:],
              op0=ALU.mult, op1=ALU.mult)
            yield
            I("vector", "tensor_tensor", [pg1, E[1]], [kt_], out=kt_[:], in0=pg1[:, F:2 * F], in1=E[1][:], op=ALU.mult)
            yield
            I("vector", "tensor_tensor", [pg1, E[2]], [kh_], out=kh_[:], in0=pg1[:, F:2 * F], in1=E[2][:], op=ALU.mult)
            yield
            copy("scalar", [pg1], [vbf], vbf[:], pg1[:, 2 * F:2 * F + V4])
            yield
            FP.put(pg1)
            pt = (yield from BP.get())
            transposes(P, pt, [(pt[0:32, hh * 128:(hh + 1) * 128], qt_[:, hh * 32:(hh + 1) * 32]) for hh in range(NH)] +
                       [(pt[0:32, 512 + hh * 128:512 + (hh + 1) * 128], kt_[:, hh * 32:(hh + 1) * 32]) for hh in range(NH)],
                       [qt_, kt_, ident])
            yield
            copy("vector", [pt], [qtT], qtT[:], vw(pt.t, 0, 32, 0, (128, NH), (1, 128)))
            yield
            copy("scalar", [pt], [ktT], ktT[:], vw(pt.t, 0, 32, 512, (128, NH), (1, 128)))
            yield
            BP.put(pt)
            pt = (yield from BP.get())
            transposes(P, pt, [(pt[0:F, 0:128], qh_[:])], [qh_, ident])
            yield
            copy("vector", [pt], [qhT], qhT[:], pt[0:F, 0:128])
            yield
            BP.put(pt)
            patt = (yield from FP.get())
            for hh in range(NH):
                mm1(P, patt, patt[:, hh * 128:(hh + 1) * 128], ktT[:, hh, :], qtT[:, hh, :], [ktT, qtT])
                yield
            I("vector", "tensor_tensor", [patt, gmask], [attm], out=attm[:], in0=vw(patt.t, 0, 128, 0, (128, NH), (1, 128)),
              in1=vw(gmask.t, 0, 128, 0, (0, NH), (1, 128)), op=ALU.mult)
            yield
            FP.put(patt)
            while st['ready'] < t:
                yield
            Sa_f, Sa_b = st['Sa_f'], st['Sa_b']
            pds = (yield from FP.get())
            pds1 = (yield from FP.get())
            mm1(P, pds, pds[0:F, 0:V4], kh_[0:64, :], vbf[0:64, :], [kh_, vbf])
            yield
            mm1(P, pds1, pds1[0:F, 0:V4], kh_[64:128, :], vbf[64:128, :], [kh_, vbf])
            yield
            Sm_f, Sm_b = Sf.next(), Sb.next()
            I("vector", "tensor_tensor", [pds, bdm], [tmpS], out=tmpS[0:F, :], in0=pds[0:F, 0:V4], in1=bdm[0:F, 0:V4], op=ALU.mult)
            yield
            FP.put(pds)
            I("vector", "scalar_tensor_tensor", [Sa_f, dec, tmpS], [Sm_f], out=Sm_f[0:F, :], in0=Sa_f[0:F, :], scalar=dec[:, 0:1],
              in1=tmpS[0:F, :], op0=ALU.mult, op1=ALU.add)
            yield
            copy("gpsimd", [Sm_f], [Sm_b], Sm_b[0:F, :], Sm_f[0:F, :])
            yield
            po = (yield from FP.get())
            for hh in range(NH):
                mm1(P, po, po[:, hh * 64:(hh + 1) * 64], attm[:, hh, :], vbf[:, hh * 64:(hh + 1) * 64], [attm, vbf],
                    start=(hh == 0), stop=False, skip=True)
                yield
            mm1(P, po, po[0:64, 0:V4], qhT[:, 0:64], Sa_b[0:F, :], [qhT, Sa_b], start=False, stop=False, skip=True)
            yield
            mm1(P, po, po[64:128, 0:V4], qhT[:, 64:128], Sm_b[0:F, :], [qhT, Sm_b], start=False, stop=True, skip=True)
            yield
            Sn_f, Sn_b = Sf.next(), Sb.next()
            I("vector", "tensor_tensor", [pds1, bdm], [tmpS], out=tmpS[0:F, :], in0=pds1[0:F, 0:V4], in1=bdm[0:F, 0:V4], op=ALU.mult)
            yield
            FP.put(pds1)
            I("vector", "scalar_tensor_tensor", [Sm_f, dec, tmpS], [Sn_f], out=Sn_f[0:F, :], in0=Sm_f[0:F, :], scalar=dec[:, 1:2],
              in1=tmpS[0:F, :], op0=ALU.mult, op1=ALU.add)
            yield
            copy("gpsimd", [Sn_f], [Sn_b], Sn_b[0:F, :], Sn_f[0:F, :])
            yield
            st['Sa_f'], st['Sa_b'] = Sn_f, Sn_b
            st['ready'] = t + 1
            I("scalar", "activation", [po], [sq], out=sq[:], in_=po[:, 0:V4], func=AF.Square, scale=0.125)
            yield
            I("vector", "tensor_reduce", [sq], [ss4], out=ss4[:], in_=vw(sq.t, 0, 128, 0, (64, NH), (1, 64)), axis=AX.X, op=ALU.add)
            yield
            I("scalar", "activation", [ss4], [rt4], out=rt4[:], in_=ss4[:], func=AF.Ln, bias=EPS, scale=1.0)
            yield
            I("scalar", "activation", [rt4], [rstd4], out=rstd4[:], in_=rt4[:], func=AF.Exp, scale=-0.5)
            yield
            I("vector", "tensor_tensor", [po, rstd4], [on], out=vw(on.t, 0, 128, 0, (64, NH), (1, 64)),
              in0=vw(po.t, 0, 128, 0, (64, NH), (1, 64)), in1=vw(rstd4.t, 0, 128, 0, (1, NH), (0, 64)), op=ALU.mult)
            yield
            FP.put(po)
            I("gpsimd", "tensor_tensor", [on, ghn], [on2], out=vw(on2.t, 0, 128, 0, (64, NH), (1, 64)),
              in0=vw(on.t, 0, 128, 0, (64, NH), (1, 64)), in1=vw(ghn.t, 0, 128, 0, (0, NH), (1, 64)), op=ALU.mult)
            yield
            I("vector", "tensor_tensor", [on2, sgl], [yb], out=yb[:], in0=on2[:], in1=sgl[:], op=ALU.mult)
            yield
            YT = ybT[b]
            pt = (yield from BP.get())
            transposes(P, pt, [(pt[:, c * 128:(c + 1) * 128], yb[:, c * 128:(c + 1) * 128]) for c in range(NJ)], [yb, ident])
            yield
            copy("scalar", [pt], [YT], YT[:], vw(pt.t, 0, 128, 0, (128, NJ), (1, 128)))
            yield
            BP.put(pt)
            P.dma("sync", bass.AP(dr["ysend"], V4 * S + t * 128, [[S, 128], [128 * S, NJ], [1, 128]]), YT[:], reads=[YT])
            yield


    def moba_chain(inst):
        O_ = OB[inst]
        qktm = O_['qktm']
        rtmp = O_['rtmp']
        vtc = O_['vtc']
        qkTs = O_['qkTs']
        for t in range(inst, NT, NMOBA):
            b = (t // NMOBA) % 2
            while ht_ready[0] <= t:
                yield
            HT = hT[t % NHT]
            pb1 = (yield from FP.get())
            proj_tm_(HT, pb1, O_BQ, 2 * V4)
            yield
            pb2 = (yield from FP.get())
            proj_tm_(HT, pb2, O_BV, V4)
            yield
            ht_done[t] += 1
            x1 = vw(pb1.t, 0, 128, 0, (64, 2 * NH), (1, 8))
            x2 = vw(pb1.t, 0, 128, 8, (64, 2 * NH), (1, 8))
            cs8 = vw(cosC.t, 0, 128, t * 8, (0, 2 * NH), (1, 8))
            sn8 = vw(sinC.t, 0, 128, t * 8, (0, 2 * NH), (1, 8))
            I("vector", "tensor_tensor", [pb1, cosC], [rtmp[0]], out=rtmp[0][:], in0=x1, in1=cs8, op=ALU.mult)
            yield
            I("vector", "tensor_tensor", [pb1, sinC], [rtmp[1]], out=rtmp[1][:], in0=x2, in1=sn8, op=ALU.mult)
            yield
            I("vector", "tensor_tensor", [rtmp[0], rtmp[1]], [qktm], out=qktm[:, :, 0:8], in0=rtmp[0][:], in1=rtmp[1][:], op=ALU.subtract)
            yield
            I("vector", "tensor_tensor", [pb1, cosC], [rtmp[2]], out=rtmp[2][:], in0=x2, in1=cs8, op=ALU.mult)
            yield
            I("vector", "tensor_tensor", [pb1, sinC], [rtmp[3]], out=rtmp[3][:], in0=x1, in1=sn8, op=ALU.mult)
            yield
            I("vector", "tensor_tensor", [rtmp[2], rtmp[3]], [qktm], out=qktm[:, :, 8:16], in0=rtmp[2][:], in1=rtmp[3][:], op=ALU.add)
            yield
            copy("scalar", [pb1], [qktm], qktm[:, :, 16:64], vw(pb1.t, 0, 128, 16, (64, 2 * NH), (1, 48)))
            yield
            FP.put(pb1)
            VC = vtc[b]
            copy("scalar", [pb2], [VC], VC[:, :, 0:64], vw(pb2.t, 0, 128, 0, (64, NH), (1, 64)))
            yield
            FP.put(pb2)
            P.dma("sync", bass.AP(dr["v_c"], t * 128 * NH * 65, [[NH * 65, 128], [1, NH * 65]]), VC[:], reads=[VC])
            yield
            QK = qkTs[b]
            for half in range(2):
                pt = (yield from BP.get())
                transposes(P, pt, [(pt[0:64, hh * 128:(hh + 1) * 128], qktm[:, half * NH + hh, :]) for hh in range(NH)],
                           [qktm, ident])
                yield
                copy(evac.next(), [pt], [QK], QK[:, half * NH:(half + 1) * NH, :], vw(pt.t, 0, 64, 0, (128, NH), (1, 128)))
                yield
                BP.put(pt)
            P.dma("sync", bass.AP(dr["qT_c"], t * 128, [[S, 64], [64 * S, NH], [1, 128]]), QK[:, 0:NH, :], reads=[QK])
            yield
            P.dma("sync", bass.AP(dr["kT_c"], t * 128, [[S, 64], [64 * S, NH], [1, 128]]), QK[:, NH:2 * NH, :], reads=[QK])
            yield


    def lru_chain():
        for t in range(NT):
            b = t % 2
            while ht_ready[0] <= t:
                yield
            HT = hT[t % NHT]
            pr = (yield from FP.get())
            for j in range(NJ):
                mm_group(P, pr, [(pr[:, j * 128:(j + 1) * 128], win[:, c, O_RX + j * 128:O_RX + (j + 1) * 128], HT[:, c, :])
                                 for c in range(8)], [HT, win])
                yield
            for j in range(NJ):
                mm_group(P, pr, [(pr[:, 256 + j * 128:256 + (j + 1) * 128], win[:, c, O_RG + j * 128:O_RG + (j + 1) * 128], HT[:, c, :])
                                 for c in range(8)], [HT, win])
                yield
            XP, XPprev = xpad[b], xpad[1 - b]
            copy("scalar", [pr], [XP], XP[:, :, 3:131], vw(pr.t, 0, 128, 0, (128, NJ), (1, 128)))
            yield
            ht_done[t] += 1
            I("scalar", "activation", [pr], [gl], out=gl[:], in_=vw(pr.t, 0, 128, 256, (128, NJ), (1, 128)), func=AF.Gelu)
            yield
            FP.put(pr)
            copy("gpsimd", [XPprev], [XP], XP[:, :, 0:3], XPprev[:, :, 128:131])
            yield
            for j in range(NJ):
                I("vector", "tensor_scalar", [XP, cw, cbv], [xc], out=xc[:, j, :], in0=XP[:, j, 0:128], scalar1=cw[:, 0, j:j + 1],
                  scalar2=cbv[:, j:j + 1], op0=ALU.mult, op1=ALU.add)
                yield
                for k in range(1, 4):
                    I("vector", "scalar_tensor_tensor", [XP, cw, xc], [xc], out=xc[:, j, :], in0=XP[:, j, k:k + 128],
                      scalar=cw[:, k, j:j + 1], in1=xc[:, j, :], op0=ALU.mult, op1=ALU.add)
                    yield
            copy("gpsimd", [xc], [xcb], xcb[:], xc[:])
            yield
            pgt = (yield from FP.get())
            for j in range(NJ):
                mm1(P, pgt, pgt[:, j * 128:(j + 1) * 128], BDa[:, j, :], xcb[:, j, :], [BDa, xcb])
                yield
                mm1(P, pgt, pgt[:, 256 + j * 128:256 + (j + 1) * 128], BDx[:, j, :], xcb[:, j, :], [BDx, xcb])
                yield
            for j in range(NJ):
                I("scalar", "activation", [pgt, nbab], [r_], out=r_[:, j, :], in_=pgt[:, j * 128:(j + 1) * 128], func=AF.Exp,
                  bias=nbab[:, j:j + 1], scale=-1.0)
                yield
                I("scalar", "activation", [pgt, nbxb], [i_], out=i_[:, j, :], in_=pgt[:, 256 + j * 128:256 + (j + 1) * 128],
                  func=AF.Exp, bias=nbxb[:, j:j + 1], scale=-1.0)
                yield
            FP.put(pgt)
            for g_ in (r_, i_):
                I("scalar", "activation", [g_], [g_], out=g_[:], in_=g_[:], func=AF.Ln, bias=1.0, scale=1.0)
                yield
                I("scalar", "activation", [g_], [g_], out=g_[:], in_=g_[:], func=AF.Exp, scale=-1.0)
                yield
            for j in range(NJ):
                I("scalar", "activation", [r_, cc], [a_], out=a_[:, j, :], in_=r_[:, j, :], func=AF.Exp, scale=cc[:, j:j + 1])
                yield
            I("scalar", "activation", [a_], [a2], out=a2[:], in_=a_[:], func=AF.Square)
            yield
            I("scalar", "activation", [a2], [ug], out=ug[:], in_=a2[:], func=AF.Ln, scale=-1.0, bias=1.0)
            yield
            I("scalar", "activation", [ug], [ug], out=ug[:], in_=ug[:], func=AF.Exp, scale=0.5)
            yield
            I("vector", "tensor_tensor", [i_, xc], [m_], out=m_[:], in0=i_[:], in1=xc[:], op=ALU.mult)
            yield
            I("gpsimd", "tensor_tensor", [ug, m_], [u_], out=u_[:], in0=ug[:], in1=m_[:], op=ALU.mult)
            yield
            HS, HSprev = hs[b], hs[1 - b]
            for j in range(NJ):
                init = 0.0 if t == 0 else HSprev[:, j, 127:128]
                I("vector", "tensor_tensor_scan", [a_, u_] + ([] if t == 0 else [HSprev]), [HS], out=HS[:, j, :],
                  data0=a_[:, j, :], data1=u_[:, j, :], initial=init, op0=ALU.mult, op1=ALU.add)
                yield
            YD = yd[b]
            I("vector", "tensor_tensor", [HS, gl], [YD], out=YD[:], in0=HS[:], in1=gl[:], op=ALU.mult)
            yield
            P.dma("sync", bass.AP(dr["ysend"], 3 * V4 * S + t * 128, [[S, 128], [128 * S, NJ], [1, 128]]), YD[:], reads=[YD])
            yield


    gens = [norm_chain(), *[mla_chain(k) for k in range(NMLA)], gla_chain(0), gla_chain(1),
            *[moba_chain(k) for k in range(NMOBA)], lru_chain()]
    alive = list(gens)
    idle_rounds = 0
    while alive:
        n0 = P.ninst
        for g in list(alive):
            try:
                next(g)
            except StopIteration:
                alive.remove(g)
        idle_rounds = idle_rounds + 1 if P.ninst == n0 else 0
        assert idle_rounds < 1000, "phase A scheduler deadlock"
    drain_dma(P)
    P.flush()
    C.close()


def phase_b(nc, P, S, dr, NH=4):
    assert NH == 2
    C = Ctx(nc, P)
    I = P.I
    NT, NQ, NB = S // 128, S // 512, S // 256
    ident = C.sb("ident", [128, 128], BF16)
    P.ident = ident
    tri = C.sb("tri", [128, 896], BF16)
    ones_f = C.sb("ones_f", [128, 64], F32)
    PTs = [Ring([C.sb(f"PT{j}_{i}", [128, 512], BF16) for i in range(3)]) for j in range(2)]
    rrows = [Ring([C.sb(f"rrow{j}_{i}", [128, 512], F32) for i in range(1)]) for j in range(2)]
    oTs = [Ring([C.sb(f"oT{j}_{i}", [65, 512], F32) for i in range(2)]) for j in range(2)]
    yts = [Ring([C.sb(f"yt{j}_{i}", [64, 512], BF16) for i in range(1)]) for j in range(2)]
    pSs = [[C.ps(f"pS{j}_{i}") for i in range(2)] for j in range(2)]
    pOs = [C.ps(f"pO{j}") for j in range(2)]
    pbc = C.ps("pbc")
    pgate = pbc
    ptr = C.ps("ptr", BF16)
    P.dma("sync", ident[:], dr["c_ident"].ap(), writes=[ident])
    P.dma("sync", tri[:], dr["c_tri"].ap(), writes=[tri])
    I("gpsimd", "memset", [], [ones_f], ones_f[:], 1.0)

    def mixer(kind):
        moba = (kind == "c")
        dq = 64 if moba else 96
        KR = dq + (16 if moba else 0)
        scale = float(dq ** -0.5)
        yoff = 2 * NH * 64 if moba else 0
        qT_d, kT_d, v_d = dr["qT_" + kind], dr["kT_" + kind], dr["v_" + kind]
        QT = [C.sb(f"QT{kind}{i}", [KR, S], BF16) for i in range(2)]
        KT = [C.sb(f"KT{kind}{i}", [KR, S], BF16) for i in range(2)]
        V = C.sb(f"V{kind}", [128, NT, NH * 65], BF16)
        for j in range(2):
            P.dma("sync", QT[j][0:dq, :], bass.AP(qT_d, j * dq * S, [[S, dq], [1, S]]), writes=[QT[j]])
            P.dma("sync", KT[j][0:dq, :], bass.AP(kT_d, j * dq * S, [[S, dq], [1, S]]), writes=[KT[j]])
        P.dma("sync", V[:], bass.AP(v_d, 0, [[NH * 65, 128], [128 * NH * 65, NT], [1, NH * 65]]), writes=[V])
        if moba:
            ksum = [C.sb(f"ksum{j}", [64, 16], F32) for j in range(2)]
            kmT = [C.sb(f"kmT{j}", [64, 16], BF16) for j in range(2)]
            gms = [C.sb(f"gm{j}", [128, 16], F32) for j in range(2)]
            mx8s = [C.sb(f"mx8{j}", [128, 8], F32) for j in range(2)]
            sels = [C.sb(f"sel{j}", [128, 16], F32) for j in range(2)]
            selbs = [Ring([C.sb(f"selb{j}_{i}", [128, 80], BF16) for i in range(2)]) for j in range(2)]
            for r_ in selbs:
                for sb_ in r_.items:
                    I("gpsimd", "memset", [], [sb_], sb_[:], 0.0)
            for kt_ in KT:
                P.dma("sync", kt_[64:80, :], dr["c_onehot"].ap(), writes=[kt_])

        def gate_chain(j):
            Q, K = QT[j], KT[j]
            gm, mx8, sel = gms[j], mx8s[j], sels[j]
            I("vector", "tensor_reduce", [K], [ksum[j]], out=ksum[j][:, 0:NB], in_=vw(K.t, 0, 64, 0, (256, NB), (1, 256)),
              axis=AX.X, op=ALU.add)
            yield
            I("vector", "tensor_scalar_mul", [ksum[j]], [kmT[j]], kmT[j][:, 0:NB], ksum[j][:, 0:NB], 1.0 / 256.0)
            yield
            for qi in range(NT):
                blk = qi // 2
                qs = slice(qi * 128, (qi + 1) * 128)
                sb_ = selbs[j].next()
                if blk <= 3:
                    I("gpsimd", "memset", [], [sb_], sb_[:, 64:80], NEGBIG)
                    I("gpsimd", "memset", [], [sb_], sb_[:, 64:64 + blk + 1], 0.0)
                    yield
                else:
                    mm1(P, pgate, pgate[:, 0:NB], Q[0:64, qs], kmT[j][:, 0:NB], [Q, kmT[j]])
                    I("gpsimd", "memset", [], [gm], gm[:], -1.0e30)
                    I("vector", "tensor_copy", [pgate], [gm], gm[:, 0:blk], pgate[:, 0:blk])
                    yield
                    I("vector", "max", [gm], [mx8], out=mx8[:], in_=gm[:])
                    yield
                    I("vector", "tensor_scalar", [gm, mx8], [sel], out=sel[:], in0=gm[:], scalar1=mx8[:, 2:3], scalar2=None,
                      op0=ALU.is_ge)
                    yield
                    I("vector", "tensor_scalar", [sel], [sb_], out=sb_[:, 64:80], in0=sel[:], scalar1=-1.0, scalar2=-NEGBIG,
                      op0=ALU.add, op1=ALU.mult)
                    yield
                    I("vector", "memset", [], [sb_], sb_[:, 64 + blk:64 + blk + 1], 0.0)
                    yield
                    yield
                    yield
                transposes(P, ptr, [(ptr[0:80, 0:128], sb_[:])], [sb_, ident])
                I("vector", "tensor_copy", [ptr], [Q], Q[64:80, qs], ptr[64:80, 0:128])
                yield

        def finalize_chain(j, hh, t, o65):
            rr = rrows[j].next()
            I("vector", "reciprocal", [o65], [rr], rr[64:65, :], o65[64:65, :])
            for _ in range(10):
                yield
            mm1(P, pbc, pbc[0:64, :], ones_f[64:65, 0:64], rr[64:65, :], [ones_f, rr])
            y_ = yts[j].next()
            I("vector", "tensor_tensor", [o65, pbc], [y_], out=y_[:], in0=o65[0:64, :], in1=pbc[0:64, :], op=ALU.mult)
            yield
            P.dma("gpsimd", bass.AP(dr["ysend"], (yoff + hh * 64) * S + t * 512, [[S, 64], [1, 512]]), y_[:], reads=[y_])
            yield

        def attn_chain(j, hh):
            Q, K = QT[j], KT[j]
            po = pOs[j]
            sb2 = pSs[j]
            for t in range(NQ):
                nk = 4 * t + 4
                qs = slice(t * 512, (t + 1) * 512)

                def s_mm(kt):
                    c0 = 128 * max(0, kt - 4 * t)
                    ps_ = sb2[kt % 2]
                    mm1(P, ps_, ps_[:, c0:512], K[:, kt * 128:(kt + 1) * 128], Q[:, t * 512 + c0:(t + 1) * 512], [K, Q])

                s_mm(0)
                yield
                for kt in range(nk):
                    if kt + 1 < nk:
                        s_mm(kt + 1)
                        yield
                    ps_ = sb2[kt % 2]
                    pt_ = PTs[j].next()
                    c0 = 128 * max(0, kt - 4 * t)
                    I("scalar", "activation", [ps_], [pt_], out=pt_[:, c0:512], in_=ps_[:, c0:512], func=AF.Exp, scale=scale)
                    yield
                    if kt >= 4 * t:
                        I("gpsimd", "tensor_tensor", [pt_, tri], [pt_], out=pt_[:, c0:c0 + 128], in0=pt_[:, c0:c0 + 128],
                          in1=tri[:, 384:512], op=ALU.mult)
                        yield
                    P.I("tensor", "matmul", [V, pt_], [po], po[0:65, c0:512], lhsT=V[:, kt, hh * 65:(hh + 1) * 65],
                        rhs=pt_[:, c0:512], start=(kt == 0), stop=(kt == nk - 1))
                    yield
                o65 = oTs[j].next()
                I("scalar", "copy", [po], [o65], out=o65[:], in_=po[0:65, :])
                yield finalize_chain(j, hh, t, o65)

        return gate_chain if moba else None, attn_chain

    _, attn_a = mixer("a")
    gate_c, attn_c = mixer("c")
    run_chains(P, [attn_a(0, 0), attn_a(1, 1), gate_c(0), gate_c(1)], "MLA attention + MoBA gating")
    run_chains(P, [attn_c(0, 0), attn_c(1, 1)], "MoBA attention")
    drain_dma(P)
    P.flush()
    C.close()


SEQ = 4096
DEPTH = 2
NHC = 2
WNAMES = ["attn_norm", "w_in", "mla_q_norm", "mla_w_uq", "mla_kv_norm", "mla_w_ukv", "gla_w_a2", "gla_b_a2",
          "gla_head_norm", "lru_conv_w", "lru_conv_b", "lru_w_a", "lru_b_a", "lru_w_x", "lru_b_x", "lru_lambda",
          "w_out", "ffn_norm", "w_ffn_in", "w_ffn_out", "final_norm"]
PAIRS = [[0, 1], [2, 3], [4, 5], [6, 7]]


def make_consts(S):
    bf = ml_dtypes.bfloat16
    c = {}
    c["c_ident"] = np.eye(128, dtype=np.float32).astype(bf)
    pos = np.arange(S, dtype=np.float32)[:, None]
    for nm, dim in (("A", 32), ("C", 16)):
        inv = (np.float32(500000.0) ** (-np.arange(0, dim, 2, dtype=np.float32) / np.float32(dim))).astype(np.float32)
        ang = (pos * inv[None, :]).astype(np.float32)
        c["c_cos" + nm] = np.cos(ang).astype(np.float32)
        c["c_sin" + nm] = np.sin(ang).astype(np.float32)
    k = np.arange(128)[:, None]
    cc = np.arange(896)[None, :]
    c["c_tri"] = ((cc - 384) >= k).astype(np.float32).astype(bf)
    j = np.arange(128)[:, None]
    i = np.arange(128)[None, :]
    same = (j // 64) == (i // 64)
    c["c_cum"] = (same & (j <= i)).astype(np.float32)
    c["c_gmask"] = c["c_cum"].astype(bf)
    c["c_chones"] = same.astype(np.float32)
    c["c_chind"] = ((np.arange(128)[:, None] // 64) == np.arange(2)[None, :]).astype(np.float32)
    c["c_bdm"] = ((np.arange(128)[:, None] // 32) == (np.arange(256)[None, :] // 64)).astype(np.float32)
    c["c_onehot"] = ((np.arange(S)[None, :] // 256) == np.arange(16)[:, None]).astype(np.float32).astype(bf)
    return c


def pack_core(inp, r, NH):
    f = lambda a: np.ascontiguousarray(np.asarray(a, dtype=np.float32))
    if NH == 4:
        return {nm: f(inp[nm]) for nm in WNAMES}
    h0 = r * NH
    F, V4, NJ = NH * 32, NH * 64, NH // 2
    w_in = np.asarray(inp["w_in"])
    cols = np.concatenate([
        np.arange(0, 416),
        O_GQ + h0 * 32 + np.arange(F), O_GK + h0 * 32 + np.arange(F), O_GV + h0 * 64 + np.arange(V4),
        O_GA + np.arange(16), O_GG + h0 * 64 + np.arange(V4),
        O_BQ + h0 * 64 + np.arange(V4), O_BK + h0 * 64 + np.arange(V4), O_BV + h0 * 64 + np.arange(V4),
        O_RX + r * NJ * 128 + np.arange(NJ * 128), O_RG + r * NJ * 128 + np.arange(NJ * 128)])
    o = {}
    for nm in ("attn_norm", "mla_q_norm", "mla_kv_norm", "gla_head_norm", "ffn_norm", "w_ffn_in", "w_ffn_out", "final_norm"):
        o[nm] = f(inp[nm])
    o["w_in"] = f(w_in[:, :, cols])
    o["mla_w_uq"] = f(np.asarray(inp["mla_w_uq"])[:, :, h0 * 96:(h0 + NH) * 96])
    o["mla_w_ukv"] = f(np.asarray(inp["mla_w_ukv"])[:, :, h0 * 128:(h0 + NH) * 128])
    o["gla_w_a2"] = f(np.asarray(inp["gla_w_a2"])[:, :, h0 * 32:(h0 + NH) * 32])
    o["gla_b_a2"] = f(np.asarray(inp["gla_b_a2"])[:, h0 * 32:(h0 + NH) * 32])
    ch = slice(r * NJ * 128, (r + 1) * NJ * 128)
    o["lru_conv_w"] = f(np.asarray(inp["lru_conv_w"])[:, :, :, ch])
    for nm in ("lru_conv_b", "lru_b_a", "lru_b_x", "lru_lambda"):
        o[nm] = f(np.asarray(inp[nm])[:, ch])
    for nm in ("lru_w_a", "lru_w_x"):
        o[nm] = f(np.asarray(inp[nm])[:, r * 2 * NJ:(r + 1) * 2 * NJ])
    nr = 4 // NH
    perm = []
    for c in range(4 * V4 // 256):
        for rr in range(nr):
            row = c * 256 + np.arange(256)
            g, idx = row // V4, row % V4
            perm.append(g * 256 + rr * V4 + idx)
    perm = np.concatenate(perm)
    o["w_out"] = f(np.asarray(inp["w_out"])[:, perm, :])
    return o


def all_gather(P, src, dst, rows, chunk):
    for c in range(rows // chunk):
        P.cccnt += 1
        cnt = P.cccnt

        def thunk(e, cnt=cnt, c=c):
            e.collective_compute("AllGather", ALU.bypass, replica_groups=PAIRS, ins=[src[c * chunk:(c + 1) * chunk, :]],
                                 outs=[dst[2 * c * chunk:2 * (c + 1) * chunk, :]]).then_inc(P.ccsem, 1)
            e.wait_ge(P.ccsem, cnt)
        P.raw("gpsimd", thunk)
    P.flush()


def build_program(S, shapes, consts, NH):
    nc = bass.Bass("TRN2", target_bir_lowering=False)
    pair = (NH == 2)
    SH = S // 2 if pair else S
    V4 = NH * 64
    dr = {}
    dr["x"] = nc.dram_tensor("x", [S, D], F32, kind="ExternalInput")
    for nm in WNAMES:
        dr[nm] = nc.dram_tensor(nm, list(shapes[nm]), F32, kind="ExternalInput")
    for nm, v in consts.items():
        dr[nm] = nc.dram_tensor(nm, list(v.shape), BF16 if v.dtype == ml_dtypes.bfloat16 else F32, kind="ExternalInput")
    dr["out"] = nc.dram_tensor("out", [SH, D], F32, kind="ExternalOutput")
    dr["qT_a"] = nc.dram_tensor("qT_a", [NH, 96, S], BF16)
    dr["kT_a"] = nc.dram_tensor("kT_a", [NH, 96, S], BF16)
    dr["v_a"] = nc.dram_tensor("v_a", [S, NH * 65], BF16)
    dr["qT_c"] = nc.dram_tensor("qT_c", [NH, 64, S], BF16)
    dr["kT_c"] = nc.dram_tensor("kT_c", [NH, 64, S], BF16)
    dr["v_c"] = nc.dram_tensor("v_c", [S, NH * 65], BF16)
    dr["yT"] = nc.dram_tensor("yT", [D, S], BF16)
    P = Prog(nc)
    if pair:
        dr["xh"] = nc.dram_tensor("xh", [SH, D], F32, kind="ExternalInput")
        dr["roff"] = nc.dram_tensor("roff", [1, 1], mybir.dt.int32, kind="ExternalInput")
        dr["ysend"] = nc.dram_tensor("ysend", [4 * V4, S], BF16)
        dr["xhalf"] = nc.dram_tensor("xhalf", [SH, D], F32)
        dr["xfull"] = nc.dram_tensor("xfull", [S, D], F32)
        P.sync_reg = nc.sync.alloc_register("roff_reg")
    else:
        dr["ysend"] = dr["yT"]
        dr["xres"] = nc.dram_tensor("xres", [S, D], F32)
    win_next, cwin = None, None
    for l in range(DEPTH):
        last = (l == DEPTH - 1)
        if pair:
            dr["xin"] = dr["x"] if l == 0 else dr["xfull"]
            dr["xin_rowmap"] = None if l == 0 else (lambda t0: ((t0 % SH) // 512) * 1024 + (t0 // SH) * 512 + (t0 % 512))
            dr["xc_in"] = dr["xh"] if l == 0 else dr["xhalf"]
            dr["xout"] = dr["out"] if last else dr["xhalf"]
        else:
            dr["xin"] = dr["x"] if l == 0 else dr["xres"]
            dr["xc_in"] = dr["xin"]
            dr["xout"] = dr["out"] if last else dr["xres"]
        if pair:
            phase_a(nc, P, S, l, dr, NH, win=win_next)
            if cwin is not None:
                cwin.close()
                cwin = None
            cw = Ctx(nc, P)
            wpre = (cw.sb("wout", [128, 8, 1024], BF16), cw.sb("wfi", [128, 8, 2 * FH], BF16))
            load_wc(P, wpre[0], wpre[1], dr, l)
            phase_b(nc, P, S, dr, NH)
            all_gather(P, dr["ysend"], dr["yT"], 4 * V4, 256)
            phase_c(nc, P, S, l, dr, SH, last, dyn=True, wpre=wpre)
            cw.close()
            if not last:
                cwin = Ctx(nc, P)
                inw = dr["w_in"].shape[2]
                win_next = cwin.sb("win", [128, 8, inw], BF16)
                load_win(P, win_next, dr, l + 1, inw)
                all_gather(P, dr["xhalf"], dr["xfull"], SH, 512)
        else:
            phase_a(nc, P, S, l, dr, NH)
            phase_b(nc, P, S, dr, NH)
            phase_c(nc, P, S, l, dr, SH, last, dyn=False)
    return nc, P


_CACHE = {}


def kernel(**inputs):
    x = np.asarray(inputs["x"], dtype=np.float32)
    B, S, _ = x.shape
    NH = NHC
    nr = 4 // NH
    consts = make_consts(S)
    packed = [pack_core(inputs, r, NH) for r in range(nr)]
    shapes = {nm: packed[0][nm].shape for nm in WNAMES}
    key = (S, NH, tuple(sorted(shapes.items())))
    if key not in _CACHE:
        _CACHE[key] = build_program(S, shapes, consts, NH)
    nc, _ = _CACHE[key]
    SH = S // nr
    in_maps = []
    for b in range(B):
        for r in range(nr):
            m = {"x": np.ascontiguousarray(x[b])}
            m.update(packed[r])
            m.update(consts)
            if nr == 2:
                m["xh"] = np.ascontiguousarray(x[b, r * SH:(r + 1) * SH])
                m["roff"] = np.array([[r * SH]], dtype=np.int32)
            in_maps.append(m)
    res = run_bass_kernel_spmd(nc, in_maps, core_ids=list(range(B * nr)))
    out = np.empty((B, S, D), dtype=np.float32)
    for b in range(B):
        for r in range(nr):
            out[b, r * SH:(r + 1) * SH] = np.asarray(res.results[b * nr + r]["out"], dtype=np.float32)
    return out
```
